# Optimizing a Trainium2 kernel written in Bass

```python
import math
import jax
import jax.numpy as jnp
from jax import lax
import numpy as np

D_MODEL = 1024
BATCH = 8
SEQ = 4096
DEPTH = 1

CHUNK = 64
N_META = 16
Q_BLOCK = 128
EPS = 1e-6
NEG_INF = -1e30

MIX_WIDTH = D_MODEL
DA_WIDTH = MIX_WIDTH // 2
DA_HEAD_DIM = 128
DA_HALF = DA_HEAD_DIM // 2
DA_HEADS = DA_WIDTH // DA_HEAD_DIM
DN_WIDTH = MIX_WIDTH - DA_WIDTH
DN_DK = 128
DN_DV = 128
DN_HEADS = DN_WIDTH // DN_DV
DN_CONV = 4
N_BUCKETS = 32
MAX_DISTANCE = 128
D_FF = ((8 * D_MODEL // 3 + 127) // 128) * 128
FFN_CONV = 3

IN_SPLITS = (DA_HEADS * DA_HEAD_DIM, DA_HEADS * DA_HEAD_DIM, DA_WIDTH,
             DN_HEADS * DN_DK, DN_HEADS * DN_DK, DN_WIDTH, DN_WIDTH, DN_HEADS, DN_HEADS)
IN_COLS = sum(IN_SPLITS)
IN_OFFSETS = tuple(int(o) for o in np.cumsum(IN_SPLITS)[:-1])

kernel_name = "hybrid_diffattn_gdn_convffn"


def rms_norm(x, g):
    xf = x.astype(jnp.float32)
    y = xf * lax.rsqrt(jnp.mean(xf * xf, axis=-1, keepdims=True) + EPS)
    return (y * g.astype(jnp.float32)).astype(x.dtype)


def l2norm(t):
    return t * lax.rsqrt(jnp.sum(t * t, axis=-1, keepdims=True) + EPS)


def causal_dwconv(x, w):
    K, C = w.shape
    return lax.conv_general_dilated(
        x, w.astype(x.dtype)[:, None, :], window_strides=(1,), padding=[(K - 1, 0)],
        dimension_numbers=('NWC', 'WIO', 'NWC'), feature_group_count=C)


def chunk_ids(pos):
    return jnp.where(pos < N_META, 0, 1 + (pos - N_META) // CHUNK)


def t5_bucket(rel):
    nb = N_BUCKETS // 2
    max_exact = nb // 2
    ret = jnp.where(rel > 0, nb, 0)
    n = jnp.abs(rel)
    nf = jnp.maximum(n, 1).astype(jnp.float32)
    large = max_exact + (jnp.log(nf / max_exact) / math.log(MAX_DISTANCE / max_exact)
                         * (nb - max_exact)).astype(jnp.int32)
    large = jnp.minimum(large, nb - 1)
    return ret + jnp.where(n < max_exact, n, large)


def diff_attention(q, k, v, lam, subln_g, rel_bias, lambda_init):
    B, Lp = q.shape[:2]
    H = DA_HEADS
    nblk = Lp // Q_BLOCK
    k = k.reshape(B, Lp, H, 2, DA_HALF)
    v = v.reshape(B, Lp, H, DA_HEAD_DIM)
    qb = jnp.moveaxis(q.reshape(B, nblk, Q_BLOCK, H, 2, DA_HALF), 1, 0)
    kpos = jnp.arange(Lp, dtype=jnp.int32)
    kcid = chunk_ids(kpos)
    table = rel_bias.astype(jnp.float32)
    scale = DA_HALF ** -0.5

    def block(args):
        qblk, bi = args
        qpos = bi * Q_BLOCK + jnp.arange(Q_BLOCK, dtype=jnp.int32)
        visible = chunk_ids(qpos)[:, None] >= kcid[None, :]
        bias = jnp.moveaxis(table[t5_bucket(kpos[None, :] - qpos[:, None])], -1, 0)
        s = jnp.einsum('bqhcd,bkhcd->bchqk', qblk, k).astype(jnp.float32) * scale + bias
        s = jnp.where(visible, s, NEG_INF)
        p = jax.nn.softmax(s, axis=-1)
        a = p[:, 0] - lam * p[:, 1]
        return jnp.einsum('bhqk,bkhd->bqhd', a.astype(v.dtype), v)

    o = lax.map(block, (qb, jnp.arange(nblk, dtype=jnp.int32)))
    o = jnp.moveaxis(o, 0, 1).reshape(B, Lp, H, DA_HEAD_DIM)
    o = rms_norm(o, subln_g) * (1.0 - lambda_init)
    return o.reshape(B, Lp, H * DA_HEAD_DIM)


def gated_deltanet(q, k, v, z, b_raw, a_raw, conv_w, A_log, dt_bias, norm_g):
    f32 = jnp.float32
    B, Lp = q.shape[:2]
    C = CHUNK
    N = Lp // C
    H = DN_HEADS
    qkv = jax.nn.silu(causal_dwconv(jnp.concatenate([q, k, v], axis=-1), conv_w))
    q, k, v = jnp.split(qkv.astype(f32), [H * DN_DK, 2 * H * DN_DK], axis=-1)
    q = l2norm(q.reshape(B, Lp, H, DN_DK)) * DN_DK ** -0.5
    k = l2norm(k.reshape(B, Lp, H, DN_DK))
    v = v.reshape(B, Lp, H, DN_DV)
    beta = jax.nn.sigmoid(b_raw.astype(f32))
    g = -jnp.exp(A_log.astype(f32)) * jax.nn.softplus(a_raw.astype(f32) + dt_bias.astype(f32))

    def to_chunks(t):
        t = t.reshape((B, N, C) + t.shape[2:])
        return jnp.moveaxis(t, 3, 1)

    q, k, v, beta, g = (to_chunks(t) for t in (q, k, v, beta, g))
    G = jnp.cumsum(g, axis=-1)
    tri = jnp.tril(jnp.ones((C, C), dtype=bool))
    strict = jnp.tril(jnp.ones((C, C), dtype=bool), -1)
    decay = jnp.exp(jnp.where(tri, G[..., :, None] - G[..., None, :], -jnp.inf))
    kk = jnp.einsum('bhnid,bhnjd->bhnij', k, k)
    M = jnp.where(strict, beta[..., None] * kk * decay, 0.0)
    T_sys = M + jnp.eye(C, dtype=f32)
    rhs = jnp.concatenate([v * beta[..., None], k * (beta * jnp.exp(G))[..., None]], axis=-1)
    sol = lax.linalg.triangular_solve(T_sys, rhs, left_side=True, lower=True, unit_diagonal=True)
    U, W = sol[..., :DN_DV], sol[..., DN_DV:]
    Aqk = jnp.einsum('bhnid,bhnjd->bhnij', q, k) * decay
    q_g = q * jnp.exp(G)[..., None]
    k_g = k * jnp.exp(G[..., -1:] - G)[..., None]
    g_last = jnp.exp(G[..., -1])

    def step(S, xs):
        qc, kc, uc, wc, ac, gc = xs
        v_new = uc - jnp.einsum('bhck,bhkv->bhcv', wc, S)
        o = jnp.einsum('bhck,bhkv->bhcv', qc, S) + jnp.einsum('bhij,bhjv->bhiv', ac, v_new)
        S = S * gc[..., None, None] + jnp.einsum('bhck,bhcv->bhkv', kc, v_new)
        return S, o

    xs = tuple(jnp.moveaxis(t, 2, 0) for t in (q_g, k_g, U, W, Aqk, g_last))
    S0 = jnp.zeros((B, H, DN_DK, DN_DV), f32)
    _, o = lax.scan(step, S0, xs)
    o = o.transpose(1, 0, 3, 2, 4).reshape(B, Lp, H, DN_DV)
    o = rms_norm(o, norm_g) * jax.nn.silu(z.astype(f32).reshape(B, Lp, H, DN_DV))
    return o.reshape(B, Lp, H * DN_DV).astype(z.dtype)


def conv_gated_ffn(u, w_up, conv_w, conv_b, w_down):
    hup = causal_dwconv(u @ w_up, conv_w) + conv_b.astype(u.dtype)
    gate, val = jnp.split(hup, 2, axis=-1)
    return (jax.nn.silu(gate) * val) @ w_down


def setup_inputs(seed: int = 0) -> dict:
    key = jax.random.key(seed)
    ks = jax.random.split(key, 20)
    f = jnp.float32
    L = DEPTH
    x = jax.random.normal(ks[0], (BATCH, SEQ, D_MODEL), f)
    meta_tokens = jax.random.normal(ks[1], (N_META, D_MODEL), f)
    rel_bias = 0.5 * jax.random.normal(ks[2], (N_BUCKETS, DA_HEADS), f)
    norm1_g = 1.0 + 0.02 * jax.random.normal(ks[3], (L, D_MODEL), f)
    w_in = jax.random.normal(ks[4], (L, D_MODEL, IN_COLS), f) * D_MODEL ** -0.5
    da_lambda = 0.1 * jax.random.normal(ks[5], (L, 4, DA_HALF), f)
    da_subln_g = 1.0 + 0.02 * jax.random.normal(ks[6], (L, DA_HEAD_DIM), f)
    dn_conv_w = jax.random.normal(ks[7], (L, DN_CONV, 2 * DN_HEADS * DN_DK + DN_WIDTH), f) * DN_CONV ** -0.5
    dn_A_log = jnp.log(jax.random.uniform(ks[8], (L, DN_HEADS), f, 1.0, 16.0))
    dt = jnp.exp(jax.random.uniform(ks[9], (L, DN_HEADS), f, math.log(1e-3), math.log(1e-1)))
    dn_dt_bias = dt + jnp.log(-jnp.expm1(-dt))
    dn_norm_g = 1.0 + 0.02 * jax.random.normal(ks[10], (L, DN_DV), f)
    w_out = jax.random.normal(ks[11], (L, MIX_WIDTH, D_MODEL), f) * MIX_WIDTH ** -0.5
    norm2_g = 1.0 + 0.02 * jax.random.normal(ks[12], (L, D_MODEL), f)
    w_up = jax.random.normal(ks[13], (L, D_MODEL, 2 * D_FF), f) * D_MODEL ** -0.5
    ffn_conv_w = jax.random.normal(ks[14], (L, FFN_CONV, 2 * D_FF), f) * FFN_CONV ** -0.5
    ffn_conv_b = 0.02 * jax.random.normal(ks[15], (L, 2 * D_FF), f)
    w_down = jax.random.normal(ks[16], (L, D_FF, D_MODEL), f) * D_FF ** -0.5
    final_norm_g = 1.0 + 0.02 * jax.random.normal(ks[17], (D_MODEL,), f)
    return {"x": x, "meta_tokens": meta_tokens, "rel_bias": rel_bias, "norm1_g": norm1_g,
            "w_in": w_in, "da_lambda": da_lambda, "da_subln_g": da_subln_g,
            "dn_conv_w": dn_conv_w, "dn_A_log": dn_A_log, "dn_dt_bias": dn_dt_bias,
            "dn_norm_g": dn_norm_g, "w_out": w_out, "norm2_g": norm2_g, "w_up": w_up,
            "ffn_conv_w": ffn_conv_w, "ffn_conv_b": ffn_conv_b, "w_down": w_down,
            "final_norm_g": final_norm_g}


def reference(x, meta_tokens, rel_bias, norm1_g, w_in, da_lambda, da_subln_g, dn_conv_w,
              dn_A_log, dn_dt_bias, dn_norm_g, w_out, norm2_g, w_up, ffn_conv_w, ffn_conv_b,
              w_down, final_norm_g):
    B, S = x.shape[0], x.shape[1]
    pad = Q_BLOCK - N_META
    meta = jnp.broadcast_to(meta_tokens.astype(x.dtype)[None], (B, N_META, D_MODEL))
    h = jnp.concatenate([meta, x, jnp.zeros((B, pad, D_MODEL), x.dtype)], axis=1)
    for l in range(DEPTH):
        lambda_init = 0.8 - 0.6 * math.exp(-0.3 * l)
        u = rms_norm(h, norm1_g[l])
        proj = u @ w_in[l]
        da_q, da_k, da_v, dn_q, dn_k, dn_v, dn_z, dn_b, dn_a = jnp.split(proj, IN_OFFSETS, axis=-1)
        lv = da_lambda[l].astype(jnp.float32)
        lam = jnp.exp(jnp.sum(lv[0] * lv[1])) - jnp.exp(jnp.sum(lv[2] * lv[3])) + lambda_init
        o_da = diff_attention(da_q, da_k, da_v, lam, da_subln_g[l], rel_bias, lambda_init)
        o_dn = gated_deltanet(dn_q, dn_k, dn_v, dn_z, dn_b, dn_a, dn_conv_w[l], dn_A_log[l],
                              dn_dt_bias[l], dn_norm_g[l])
        h = h + jnp.concatenate([o_da.astype(h.dtype), o_dn.astype(h.dtype)], axis=-1) @ w_out[l]
        u = rms_norm(h, norm2_g[l])
        h = h + conv_gated_ffn(u, w_up[l], ffn_conv_w[l], ffn_conv_b[l], w_down[l])
    h = rms_norm(h, final_norm_g)
    return h[:, N_META:N_META + S]
```

```python
import math
from contextlib import ExitStack
import numpy as np
import concourse.bass as bass
import concourse.mybir as mybir
from concourse.bass_utils import run_bass_kernel_spmd

F32 = mybir.dt.float32
BF16 = mybir.dt.bfloat16
AF = mybir.ActivationFunctionType
ALU = mybir.AluOpType

D = 1024
NT_FULL = 33
EPS = 1e-6
IN_COLS = 3592
D_FF = 2816
NEG = -1.0e30

P_GSUB = 0
P_GDN = 1
P_ALOG = 2
P_DTB = 6
P_C15 = 10
P_CW = 14
P_FW = 62
P_FB = 194
P_LAM = 238
NPRM = 494
C_ID = 0
C_J = 128
C_U = 256
C_MSL = 384
C_MUI = 512
C_ONE = 640
C_E1 = 768
C_PMSL = 1280
NCST = 1408
FV_OFF = 255


PSUM_NAMES = {"a_pT", "a_pm", "a_pba", "b_p", "b_pG", "c_pS", "c_pO", "c_pZ", "c_pN",
              "d_pm", "d_pT", "e_pg", "e_pv", "e_pd"}


class Op:
    __slots__ = ("eng", "fn", "deps", "is_dma", "semkey", "signal", "sig", "phase")


class Prog:
    ENGS = ("pe", "act", "dve", "pool", "sp")

    def __init__(self, nc, stack):
        self.nc = nc
        self.stack = stack
        self.esem = {e: stack.enter_context(nc.semaphore("s_" + e)) for e in self.ENGS}
        self.ecount = {e: 0 for e in self.ENGS}
        self.dsem = {}
        self.dcount = {}
        self.waited = {e: {} for e in self.ENGS}
        self.lastw = {}
        self.readers = {}
        self.ops = []
        self.phase = 0
        self.nops = 0

    def op(self, eng, fn, reads=(), writes=(), dma=None):
        o = Op()
        o.eng = eng
        o.fn = fn
        o.is_dma = dma is not None
        o.semkey = dma
        o.signal = o.is_dma
        o.sig = None
        o.phase = self.phase
        deps = []
        for k in reads:
            w = self.lastw.get(k)
            if w is not None and w.phase == self.phase:
                deps.append(w)
            if (k if isinstance(k, str) else k[0]) in PSUM_NAMES:
                for r in self.readers.get(k, ()):
                    if r.phase == self.phase and r.eng != eng:
                        deps.append(r)
        for k in writes:
            w = self.lastw.get(k)
            if w is not None and w.phase == self.phase:
                deps.append(w)
            for r in self.readers.get(k, ()):
                if r.phase == self.phase:
                    deps.append(r)
        for k in reads:
            self.readers.setdefault(k, []).append(o)
        for k in writes:
            self.lastw[k] = o
            self.readers[k] = []
        o.deps = [d for d in dict.fromkeys(deps) if d is not o]
        self.ops.append(o)
        return o

    def I(self, eng, name, reads=(), writes=(), **kw):
        return self.op(eng, lambda e: getattr(e, name)(**kw), reads, writes)

    def dma(self, out, in_, reads=(), writes=(), sem="d0", eng="sp"):
        if sem not in self.dsem:
            self.dsem[sem] = self.stack.enter_context(self.nc.semaphore("d_" + sem))
            self.dcount[sem] = 0
        return self.op(eng, lambda e: e.dma_start(out=out, in_=in_), reads, writes, dma=sem)

    def flush(self, final=False):
        ops = self.ops
        self.ops = []
        last = {}
        for o in ops:
            last[o.eng] = o
            for d in o.deps:
                if not d.is_dma and not (d.eng == "pe" and o.eng == "pe"):
                    d.signal = True
        for o in last.values():
            o.signal = True
        start_e = dict(self.ecount)
        start_d = dict(self.dcount)
        for o in ops:
            if o.is_dma:
                self.dcount[o.semkey] += 16
                o.sig = self.dcount[o.semkey]
            elif o.signal:
                self.ecount[o.eng] += 1
                o.sig = self.ecount[o.eng]
        handles = {}
        with self.nc.Block() as block:
            reg = {"pe": block.tensor, "act": block.scalar, "dve": block.vector,
                   "pool": block.gpsimd, "sp": block.sync}
            for e in self.ENGS:
                mine = [o for o in ops if o.eng == e]

                def body(eh, e=e, mine=mine):
                    wt = self.waited[e]
                    for e2 in self.ENGS:
                        if e2 != e and start_e[e2] > wt.get(e2, 0):
                            eh.wait_ge(self.esem[e2], start_e[e2])
                            wt[e2] = start_e[e2]
                    for k, v in start_d.items():
                        if v > wt.get("d_" + k, 0):
                            eh.wait_ge(self.dsem[k], v)
                            wt["d_" + k] = v
                    for o in mine:
                        for d in o.deps:
                            if d.is_dma:
                                key = "d_" + d.semkey
                                if d.sig > wt.get(key, 0):
                                    eh.wait_ge(self.dsem[d.semkey], d.sig)
                                    wt[key] = d.sig
                            else:
                                if d.eng == "pe" and e == "pe":
                                    continue
                                if d.sig > wt.get(d.eng, 0):
                                    eh.wait_ge(self.esem[d.eng], d.sig)
                                    wt[d.eng] = d.sig
                        ins = o.fn(eh)
                        if o.is_dma:
                            ins.then_inc(self.dsem[o.semkey], 16)
                        elif o.signal:
                            ins.then_inc(self.esem[e], 1)
                    if final:
                        for k, v in self.dcount.items():
                            if v > wt.get("d_" + k, 0):
                                eh.wait_ge(self.dsem[k], v)
                                wt["d_" + k] = v
                        for e2 in self.ENGS:
                            if e2 != e and self.ecount[e2] > wt.get(e2, 0):
                                eh.wait_ge(self.esem[e2], self.ecount[e2])
                                wt[e2] = self.ecount[e2]

                reg[e](body)
        self.nops += len(ops)
        self.phase += 1


def dram_ap(t, off, pattern):
    return bass.AP(t, off, [list(p) for p in pattern])


def build(NT=NT_FULL, debug=False, stop_after=None):
    T = NT * 128
    nc = bass.Bass("TRN2", target_bir_lowering=False)
    okind = "ExternalOutput" if debug else None

    def dt_(name, shape, dt, kind=None):
        if kind is None:
            return nc.dram_tensor(name, shape, dt)
        return nc.dram_tensor(name, shape, dt, kind=kind)

    hin = dt_("hin", [T, D], F32, "ExternalInput")
    w_in = dt_("w_in", [D, IN_COLS], F32, "ExternalInput")
    w_out = dt_("w_out", [D, D], F32, "ExternalInput")
    w_up = dt_("w_up", [D, 2 * D_FF], F32, "ExternalInput")
    w_down = dt_("w_down", [D_FF, D], F32, "ExternalInput")
    gb = dt_("gb", [128, 3 * D], F32, "ExternalInput")
    prm = dt_("prm", [128, NPRM], F32, "ExternalInput")
    cst = dt_("cst", [128, NCST], F32, "ExternalInput")
    relb = dt_("relb", [32, 4], F32, "ExternalInput")
    out = dt_("out", [(NT - 1) * 128, D], F32, "ExternalOutput")

    qT_da = dt_("qT_da", [4, 128, T], BF16, okind)
    kT_da = dt_("kT_da", [4, 128, T], BF16, okind)
    v_da = dt_("v_da", [T, 512], BF16, okind)
    qkvT = dt_("qkvT", [12, 128, T], F32, okind)
    szT = dt_("szT", [4, 128, T], F32, okind)
    ba = dt_("ba", [T, 8], F32, okind)
    oT = dt_("oT", [8, 128, T], BF16, okind)
    hmid = dt_("hmid", [T, D], F32, okind)
    u2T = dt_("u2T", [8, 128, T], BF16, okind)
    fvec = dt_("fvec", [4, 512], F32, okind)

    with ExitStack() as top:
        P = Prog(nc, top)
        prm_t = top.enter_context(nc.sbuf_tensor("prm_t", [128, NPRM], F32))
        cst_t = top.enter_context(nc.sbuf_tensor("cst_t", [128, NCST], F32))
        idb = top.enter_context(nc.sbuf_tensor("idb", [128, 128], BF16))
        oneb = top.enter_context(nc.sbuf_tensor("oneb", [128, 128], BF16))
        P.dma(prm_t[:, :], prm.ap()[:, :], writes=["prm"], sem="l_prm")
        P.dma(cst_t[:, :], cst.ap()[:, :], writes=["cst"], sem="l_cst")
        epsc = top.enter_context(nc.sbuf_tensor("epsc", [128, 2], F32))
        P.op("dve", lambda e: e.memset(epsc[:, 0:1], EPS), writes=["epsc"])
        P.op("dve", lambda e: e.memset(epsc[:, 1:2], 128.0 * EPS), writes=["epsc"])
        P.op("dve", lambda e: e.tensor_copy(out=idb[:, :], in_=cst_t[:, C_ID:C_ID + 128]),
             reads=["cst"], writes=["idb"])
        P.op("dve", lambda e: e.tensor_copy(out=oneb[:, :], in_=cst_t[:, C_ONE:C_ONE + 128]),
             reads=["cst"], writes=["oneb"])

        env = dict(nc=nc, P=P, NT=NT, T=T, hin=hin, w_in=w_in, w_out=w_out, w_up=w_up,
                   w_down=w_down, gb=gb, prm=prm_t, cst=cst_t, relb=relb, out=out,
                   qT_da=qT_da, kT_da=kT_da, v_da=v_da, qkvT=qkvT, szT=szT, ba=ba, oT=oT,
                   hmid=hmid, u2T=u2T, fvec=fvec, idb=idb, oneb=oneb, epsc=epsc)
        phases = [("A", phase_a), ("B", phase_b), ("C", phase_c), ("D", phase_d), ("E", phase_e)]
        for i, (name, fn) in enumerate(phases):
            lastp = (i == len(phases) - 1) or (stop_after == name)
            if name == "C":
                env["wup"] = top.enter_context(nc.sbuf_tensor("e_wup", [128, 8, 2 * D_FF], BF16))
            fn(env)
            P.flush(final=lastp)
            if lastp:
                break
    return nc


def load_weight_bf16(P, nc, wt, w_dram, nk, ncols, key, sem, kgroup=4, step=2048):
    for k0 in range(0, nk, kgroup):
        k1 = min(nk, k0 + kgroup)
        for c0 in range(0, ncols, step):
            c1 = min(ncols, c0 + step)
            P.dma(wt[:, k0:k1, c0:c1],
                  w_dram.ap()[k0 * 128:k1 * 128, c0:c1].rearrange("(k p) c -> p k c", p=128),
                  writes=[key], sem=sem, eng="pool")


def rmsnorm_rows(P, x_t, xkey, ss_t, sskey, rstd_t, rkey, junk_t, jkey, epsc, n=D):
    P.op("act", lambda e: e.activation(out=junk_t, in_=x_t, func=AF.Square, accum_out=ss_t),
         reads=[xkey], writes=[jkey, sskey])
    P.op("act", lambda e: e.activation(out=rstd_t, in_=ss_t, func=AF.Sqrt, bias=epsc[:, 0:1], scale=1.0 / n),
         reads=[sskey, "epsc"], writes=[rkey])
    P.op("dve", lambda e: e.reciprocal(out=rstd_t, in_=rstd_t),
         reads=[rkey], writes=[rkey])


def phase_a(env):
    nc, P, NT, T = env["nc"], env["P"], env["NT"], env["T"]
    hin, idb = env["hin"], env["idb"]
    with ExitStack() as st:
        S = lambda name, shape, dt: st.enter_context(nc.sbuf_tensor(name, shape, dt))
        PS = lambda name, shape, dt: st.enter_context(nc.psum_tensor(name, shape, dt))
        win = S("a_win", [128, 8, IN_COLS], BF16)
        g1 = S("a_g1", [128, D], F32)
        ht = [S(f"a_h{i}", [128, D], F32) for i in range(2)]
        junk = S("a_junk", [128, D], BF16)
        ss = [S(f"a_ss{i}", [128, 1], F32) for i in range(2)]
        rstd = [S(f"a_rs{i}", [128, 1], F32) for i in range(2)]
        ub = [S(f"a_u{i}", [128, D], BF16) for i in range(2)]
        uT = [S(f"a_uT{i}", [128, 8, 512], BF16) for i in range(2)]
        evb = [S(f"a_evb{i}", [128, 512], BF16) for i in range(3)]
        evf = [S(f"a_evf{i}", [128, 512], F32) for i in range(3)]
        bat = [S(f"a_ba{i}", [128, 8], F32) for i in range(2)]
        pT = [PS(f"a_pT{i}", [128, 8, 128], BF16) for i in range(2)]
        pm = [PS(f"a_pm{i}", [128, 512], F32) for i in range(4)]
        pba = PS("a_pba", [128, 8], F32)

        P.dma(g1[:, :], env["gb"].ap()[:, 0:D], writes=["a_g1"], sem="l_g1")
        WBLK = [(0, 1536), (1536, IN_COLS)]
        for bi_, (cA, cB) in enumerate(WBLK):
            for k0 in (0, 4):
                P.dma(win[:, k0:k0 + 4, cA:cB],
                      env["w_in"].ap()[k0 * 128:(k0 + 4) * 128, cA:cB].rearrange("(k p) c -> p k c", p=128),
                      writes=[("a_win", bi_)], sem=f"wA{bi_}", eng="pool")

        def wkeys(cA, cB):
            return [("a_win", i) for i, (x0, x1) in enumerate(WBLK) if x0 < cB and cA < x1]

        nsup = (NT + 3) // 4
        pmi = 0
        evi = 0

        def load_tile(t):
            P.dma(ht[t % 2][:, :], hin.ap()[t * 128:(t + 1) * 128, :],
                  writes=[("a_h", t % 2)], sem=f"a_h{t % 2}")

        def prep_tile(t):
            s_, j = t // 4, t % 4
            us_ = uT[s_ % 2]
            b = t % 2
            if t + 1 < NT:
                load_tile(t + 1)
            rmsnorm_rows(P, ht[b][:, :], ("a_h", b), ss[b][:, :], ("a_ss", b),
                         rstd[b][:, :], ("a_rs", b), junk[:, :], "a_junk", env["epsc"])
            P.I("dve", "scalar_tensor_tensor", reads=[("a_h", b), ("a_rs", b), "a_g1"], writes=[("a_u", b)],
                out=ub[b][:, :], in0=ht[b][:, :], scalar=rstd[b][:, 0:1], in1=g1[:, :], op0=ALU.mult, op1=ALU.mult)

        def prep_tile_b(t):
            s_, j = t // 4, t % 4
            us_ = uT[s_ % 2]
            b = t % 2
            for kc in range(8):
                P.I("pe", "transpose", reads=[("a_u", b), "idb"], writes=[("a_pT", b)],
                    out=pT[b][:, kc, :], in_=ub[b][:, kc * 128:(kc + 1) * 128], identity=idb[:, :])
            P.I("act", "copy", reads=[("a_pT", b)], writes=[("a_uT", s_ % 2, j)],
                out=us_[:, :, j * 128:(j + 1) * 128], in_=pT[b][:, :, :])

        load_tile(0)
        for t in range(min(4, NT)):
            prep_tile(t)
            prep_tile_b(t)
        for s in range(nsup):
            t0 = s * 4
            ntile = min(4, NT - t0)
            W = ntile * 128
            us = uT[s % 2]
            nxt_tiles = list(range(t0 + 4, min(t0 + 8, NT)))
            gcount = 0
            UK = [("a_uT", s % 2, j) for j in range(ntile)]
            c0 = t0 * 128
            fm = []
            for h in range(4):
                fm.append((h * 128, "q", h))
            for h in range(4):
                fm.append((512 + h * 128, "k", h))
            for ct in range(12):
                fm.append((1536 + ct * 128, "x", ct))
            for h in range(4):
                fm.append((3072 + h * 128, "z", h))
            for (col, kind, idx) in fm:
                gcount += 1
                if gcount % 6 == 1 and nxt_tiles:
                    prep_tile(nxt_tiles[0])
                if gcount % 6 == 4 and nxt_tiles:
                    prep_tile_b(nxt_tiles.pop(0))
                pb = pmi % 4
                pmi += 1
                for kc in range(8):
                    P.op("pe", lambda e, pb=pb, kc=kc, col=col, us=us, W=W: e.matmul(
                        out=pm[pb][:, 0:W], lhsT=win[:, kc, col:col + 128], rhs=us[:, kc, 0:W],
                        start=(kc == 0), stop=(kc == 7)),
                        reads=UK + wkeys(col, col + 128), writes=[("a_pm", pb)])
                eb = evi % 3
                evi += 1
                if kind in ("q", "k"):
                    dst = env["qT_da"] if kind == "q" else env["kT_da"]
                    P.op("act", lambda e, pb=pb, eb=eb, W=W: e.copy(out=evb[eb][:, 0:W], in_=pm[pb][:, 0:W]),
                         reads=[("a_pm", pb)], writes=[("a_evb", eb)])
                    P.dma(dst.ap()[idx, :, c0:c0 + W], evb[eb][:, 0:W], reads=[("a_evb", eb)],
                          writes=[(kind + "T_da", idx, s)], sem=f"a_sb{eb}")
                elif kind == "x":
                    P.op("dve", lambda e, pb=pb, eb=eb, W=W: e.tensor_copy(out=evf[eb][:, 0:W], in_=pm[pb][:, 0:W]),
                         reads=[("a_pm", pb)], writes=[("a_evf", eb)])
                    P.dma(env["qkvT"].ap()[idx, :, c0:c0 + W], evf[eb][:, 0:W], reads=[("a_evf", eb)],
                          writes=[("qkvT", idx, s)], sem=f"a_sf{eb}")
                else:
                    P.op("act", lambda e, pb=pb, eb=eb, W=W: e.activation(
                        out=evf[eb][:, 0:W], in_=pm[pb][:, 0:W], func=AF.Silu),
                        reads=[("a_pm", pb)], writes=[("a_evf", eb)])
                    P.dma(env["szT"].ap()[idx, :, c0:c0 + W], evf[eb][:, 0:W], reads=[("a_evf", eb)],
                          writes=[("szT", idx, s)], sem=f"a_sf{eb}")
            while nxt_tiles:
                t_ = nxt_tiles.pop(0)
                prep_tile(t_)
                prep_tile_b(t_)
            for j in range(ntile):
                t = t0 + j
                pb = pmi % 4
                pmi += 1
                for kc in range(8):
                    P.op("pe", lambda e, pb=pb, kc=kc, us=us, j=j: e.matmul(
                        out=pm[pb][:, :], lhsT=us[:, kc, j * 128:(j + 1) * 128], rhs=win[:, kc, 1024:1536],
                        start=(kc == 0), stop=(kc == 7)),
                        reads=[("a_uT", s % 2, j)] + wkeys(1024, 1536), writes=[("a_pm", pb)])
                eb = evi % 3
                evi += 1
                P.op("act", lambda e, pb=pb, eb=eb: e.copy(out=evb[eb][:, :], in_=pm[pb][:, :]),
                     reads=[("a_pm", pb)], writes=[("a_evb", eb)])
                P.dma(env["v_da"].ap()[t * 128:(t + 1) * 128, :], evb[eb][:, :], reads=[("a_evb", eb)],
                      writes=[("v_da", t)], sem=f"a_sb{eb}")
                for kc in range(8):
                    P.op("pe", lambda e, kc=kc, us=us, j=j: e.matmul(
                        out=pba[:, :], lhsT=us[:, kc, j * 128:(j + 1) * 128], rhs=win[:, kc, 3584:3592],
                        start=(kc == 0), stop=(kc == 7)),
                        reads=[("a_uT", s % 2, j)] + wkeys(3584, 3592), writes=["a_pba"])
                P.op("dve", lambda e, t=t: e.tensor_copy(out=bat[t % 2][:, :], in_=pba[:, :]),
                     reads=["a_pba"], writes=[("a_bat", t % 2)])
                P.dma(env["ba"].ap()[t * 128:(t + 1) * 128, :], bat[t % 2][:, :], reads=[("a_bat", t % 2)],
                      writes=[("ba", t)], sem=f"a_sba{t % 2}")


def phase_b(env):
    nc, P, NT, T = env["nc"], env["P"], env["NT"], env["T"]
    prm, cst, epsc, oneb, idb = env["prm"], env["cst"], env["epsc"], env["oneb"], env["idb"]
    H4 = range(4)
    DN = F32
    DO = BF16
    DK = BF16
    DX = F32
    with ExitStack() as st:
        S = lambda name, shape, dt: st.enter_context(nc.sbuf_tensor(name, shape, dt))
        PS = lambda name, shape, dt: st.enter_context(nc.psum_tensor(name, shape, dt))
        ba_all = S("b_ba", [128, NT, 8], F32)
        beta = S("b_beta", [128, NT, 4], F32)
        nbeta = S("b_nbeta", [128, NT, 4], F32)
        gg = S("b_g", [128, NT, 4], F32)
        etmp = S("b_etmp", [128, NT], F32)
        negA = S("b_negA", [128, 4], F32)
        pmsl4 = S("b_pmsl4", [128, 4, 128], F32)
        mui4 = S("b_mui4", [128, 4, 128], F32)
        id4 = S("b_id4", [128, 4, 128], F32)
        xin = [S(f"b_xin{i}", [128, 515], F32) for i in range(2)]
        ycv = [S(f"b_y{i}", [128, 512], F32) for i in range(2)]
        ys = [S(f"b_ys{i}", [128, 512], F32) for i in range(8)]
        qkbs = [S(f"b_qkb{i}", [128, 12, 512], DK) for i in range(2)]
        sq = [S(f"b_sq{i}", [128, 512], BF16) for i in range(2)]
        rn = [S(f"b_rn{i}", [128, 512], F32) for i in range(2)]
        szs = [S(f"b_sz{i}", [128, 4, 512], F32) for i in range(3)]
        def bufset(n):
            d = {}
            d["n"] = n
            d["Gt"] = S(f"b_Gt{n}", [128, 8], F32)
            d["gU"] = S(f"b_gU{n}", [128, 4, 128], F32)
            d["tA"] = S(f"b_tA{n}", [128, 4, 128], F32)
            d["tAT"] = S(f"b_tAT{n}", [128, 4, 128], F32)
            d["Dm"] = S(f"b_Dm{n}", [128, 4, 128], F32)
            d["DT"] = S(f"b_DT{n}", [128, 4, 128], F32)
            d["eGr"] = S(f"b_eGr{n}", [128, 4, 128], F32)
            d["X"] = [S(f"b_X{n}{i}", [128, 4, 128], DX) for i in range(2)]
            d["XT"] = [S(f"b_XT{n}{i}", [128, 4, 128], DX) for i in range(2)]
            d["IX"] = S(f"b_IX{n}", [128, 4, 128], DN)
            d["Pt"] = [S(f"b_Pt{n}{i}", [128, 4, 128], DN) for i in range(2)]
            d["kbgn"] = S(f"b_kbgn{n}", [128, 4, 128], DN)
            return d

        def hoset(n):
            d = {}
            d["n"] = n
            d["sc"] = S(f"b_sc{n}", [128, 16], F32)
            d["Ptf"] = S(f"b_Ptf{n}", [128, 4, 128], DN)
            d["AqkT"] = S(f"b_AqkT{n}", [128, 4, 128], DO)
            d["kg"] = S(f"b_kg{n}", [128, 4, 128], DO)
            d["vb"] = S(f"b_vb{n}", [128, 4, 128], DN)
            d["WTn"] = S(f"b_WTn{n}", [128, 4, 128], DN)
            d["qgT"] = S(f"b_qgT{n}", [128, 4, 128], DO)
            return d

        HO = [hoset(i) for i in range(4)]
        BS = [bufset(0), bufset(1)]
        vnew = S("b_vnew", [128, 4, 128], DO)
        St = S("b_S", [128, 4, 128], F32)
        Sb = S("b_Sb", [128, 4, 128], DO)
        osq = S("b_osq", [128, 4, 128], BF16)
        rno = S("b_rno", [128, 4, 128], F32)
        t1 = S("b_t1", [128, 4, 128], F32)
        ob = [S(f"b_ob{i}", [128, 4, 128], BF16) for i in range(2)]
        pb = [PS(f"b_p{i}", [128, 4, 128], F32) for i in range(7)]
        pcount = [0]

        live = [False] * 7

        def nxt():
            for _ in range(7):
                i = pcount[0] % 7
                pcount[0] += 1
                if not live[i]:
                    live[i] = True
                    return pb[i], ("b_p", i)
            raise AssertionError("all GDN PSUM banks are live")

        def done(key):
            live[key[1]] = False

        def bfv(p, dt=BF16):
            return p[:, :, :].bitcast(BF16) if dt == BF16 else p[:, :, :]

        ONES = cst[:, C_ONE:C_ONE + 128]
        UM = cst[:, C_U:C_U + 128]
        IDF = cst[:, C_ID:C_ID + 128]

        for h in H4:
            P.I("pool", "tensor_copy", reads=["cst"], writes=["b_pmsl4"], out=pmsl4[:, h, :], in_=cst[:, C_PMSL:C_PMSL + 128])
            P.I("pool", "tensor_copy", reads=["cst"], writes=["b_mui4"], out=mui4[:, h, :], in_=cst[:, C_MUI:C_MUI + 128])
            P.I("pool", "tensor_copy", reads=["cst"], writes=["b_id4"], out=id4[:, h, :], in_=IDF)
        P.I("pool", "memset", writes=["b_S"], ap=St[:, :, :], constant=0.0)
        P.I("pool", "memset", writes=["b_Sb"], ap=Sb[:, :, :], constant=0.0)
        P.dma(ba_all[:, :, :], env["ba"].ap()[:, :].rearrange("(t p) c -> p t c", p=128), writes=["b_ba"], sem="b_l0")
        P.I("act", "activation", reads=["b_ba"], writes=["b_beta"], out=beta[:, :, :], in_=ba_all[:, :, 0:4], func=AF.Sigmoid)
        P.I("dve", "tensor_scalar", reads=["b_beta"], writes=["b_nbeta"], out=nbeta[:, :, :], in0=beta[:, :, :],
            scalar1=-1.0, scalar2=None, op0=ALU.mult)
        P.I("act", "activation", reads=["prm"], writes=["b_negA"], out=negA[:, :], in_=prm[:, P_ALOG:P_ALOG + 4], func=AF.Exp)
        P.I("dve", "tensor_scalar", reads=["b_negA"], writes=["b_negA"], out=negA[:, :], in0=negA[:, :],
            scalar1=-1.0, scalar2=None, op0=ALU.mult)
        for h in H4:
            P.I("act", "activation", reads=["b_ba", "prm"], writes=["b_etmp"], out=etmp[:, :], in_=ba_all[:, :, 4 + h],
                func=AF.Exp, bias=prm[:, P_DTB + h:P_DTB + h + 1], scale=1.0)
            P.I("act", "activation", reads=["b_etmp", "cst"], writes=["b_etmp"], out=etmp[:, :], in_=etmp[:, :],
                func=AF.Ln, bias=cst[:, C_ONE:C_ONE + 1], scale=1.0)
            P.I("dve", "tensor_scalar", reads=["b_etmp", "b_negA"], writes=["b_g"], out=gg[:, :, h], in0=etmp[:, :],
                scalar1=negA[:, h:h + 1], scalar2=None, op0=ALU.mult)

        nsup = (NT + 3) // 4
        xctr = [0]

        def conv_gen(s):
            t0 = s * 4
            ntile = min(4, NT - t0)
            W = ntile * 128
            c0 = t0 * 128
            sp = s % 2
            qkb = qkbs[sp]
            P.dma(szs[s % 3][:, :, 0:W], env["szT"].ap()[:, :, c0:c0 + W].rearrange("k p c -> p k c"), writes=[("b_sz", s % 3)], sem=f"b_l1{s % 3}")
            for ct in range(12):
                xb = xctr[0] % 2
                xctr[0] += 1
                xk = ("b_xin", xb)
                if s == 0:
                    P.I("pool", "memset", writes=[xk], ap=xin[xb][:, 0:3], constant=0.0)
                    P.dma(xin[xb][:, 3:3 + W], env["qkvT"].ap()[ct, :, 0:W], writes=[xk], sem=f"b_lx{xb}")
                else:
                    P.dma(xin[xb][:, 0:3 + W], env["qkvT"].ap()[ct, :, c0 - 3:c0 + W], writes=[xk], sem=f"b_lx{xb}")
                y = ycv[xb]
                yk = ("b_y", xb)
                wc = lambda tap: prm[:, P_CW + ct * 4 + tap:P_CW + ct * 4 + tap + 1]
                P.I("act", "activation", reads=[xk, "prm"], writes=[yk], out=y[:, 0:W], in_=xin[xb][:, 3:3 + W],
                    func=AF.Identity, scale=wc(3))
                yield
                for tap in (2, 1, 0):
                    P.I("dve", "scalar_tensor_tensor", reads=[xk, "prm", yk], writes=[yk], out=y[:, 0:W],
                        in0=xin[xb][:, tap:tap + W], scalar=wc(tap), in1=y[:, 0:W], op0=ALU.mult, op1=ALU.add)
                yield
                if ct >= 8:
                    P.I("act", "activation", reads=[yk], writes=[("b_qkb", sp, ct)], out=qkb[:, ct, 0:W], in_=y[:, 0:W], func=AF.Silu)
                else:
                    P.I("act", "activation", reads=[yk], writes=[("b_ys", ct)], out=ys[ct][:, 0:W], in_=y[:, 0:W], func=AF.Silu)
                yield
            for ct in range(8):
                xb = ct % 2
                ysb = ys[ct]
                ysk = ("b_ys", ct)
                qk_ = ("b_qkb", sp, ct)
                P.I("pool", "tensor_tensor", reads=[ysk], writes=[("b_sq", xb)], out=sq[xb][:, 0:W],
                    in0=ysb[:, 0:W], in1=ysb[:, 0:W], op=ALU.mult)
                yield
                pp, pk = nxt()
                P.I("pe", "matmul", reads=["oneb", ("b_sq", xb)], writes=[pk], out=pp[:, 0:ntile, :],
                    lhsT=oneb[:, :], rhs=sq[xb][:, 0:W], start=True, stop=True)
                yield
                P.I("act", "activation", reads=[pk, "epsc"], writes=[("b_rn", xb)], out=rn[xb][:, 0:W],
                    in_=pp[:, 0:ntile, :], func=AF.Ln, bias=epsc[:, 0:1], scale=1.0)
                done(pk)
                P.I("act", "activation", reads=[("b_rn", xb)], writes=[("b_rn", xb)], out=rn[xb][:, 0:W],
                    in_=rn[xb][:, 0:W], func=AF.Exp, scale=-0.5)
                yield
                if ct < 4:
                    P.I("dve", "scalar_tensor_tensor", reads=[ysk, ("b_rn", xb)], writes=[qk_], out=qkb[:, ct, 0:W],
                        in0=ysb[:, 0:W], scalar=128 ** -0.5, in1=rn[xb][:, 0:W], op0=ALU.mult, op1=ALU.mult)
                else:
                    P.I("dve", "tensor_tensor", reads=[ysk, ("b_rn", xb)], writes=[qk_], out=qkb[:, ct, 0:W],
                        in0=ysb[:, 0:W], in1=rn[xb][:, 0:W], op=ALU.mult)
                yield

        def drive(gens):
            gens = list(gens)
            while gens:
                for g in list(gens):
                    try:
                        next(g)
                    except StopIteration:
                        gens.remove(g)

        drive([conv_gen(0)])
        state_b = dict(pair=0, prev=None)
        for s in range(nsup):
            t0 = s * 4
            ntile = min(4, NT - t0)
            W = ntile * 128
            c0 = t0 * 128
            sp = s % 2
            qkb = qkbs[sp]
            QK = [("b_qkb", sp, ct) for ct in range(12)]
            nextconv = [conv_gen(s + 1)] if s + 1 < nsup else []

            def chunk_prep(j, bs, ho, qkb=qkb, QK=QK, t0=t0):
                t = t0 + j
                n = bs["n"]
                hn = ho["n"]
                cs = slice(j * 128, (j + 1) * 128)
                Gt, gU, tA, tAT, Dm, DT, eGr = (bs[k] for k in ("Gt", "gU", "tA", "tAT", "Dm", "DT", "eGr"))
                X, XT, IX, Pt, kbgn = (bs[k] for k in ("X", "XT", "IX", "Pt", "kbgn"))
                sc, AqkT, kg, vb, WTn, qgT, Ptf = (ho[k] for k in ("sc", "AqkT", "kg", "vb", "WTn", "qgT", "Ptf"))
                KW = lambda *a: ("b_cs", n) + a
                KH = lambda *a: ("b_ho", hn) + a
                HN = ("sc", "AqkT", "kg", "vb", "WTn", "qgT", "Ptf")
                K = lambda *a: (KH(*a) if a[0] in HN else KW(*a))
                pGl, kGl = nxt()
                pG = pGl[:, 0, 0:8]
                P.I("pe", "matmul", reads=["cst", "b_g"], writes=[kGl], out=pG[:, 0:4], lhsT=UM, rhs=gg[:, t, :], start=True, stop=True)
                P.I("pe", "matmul", reads=["cst", "b_g"], writes=[kGl], out=pG[:, 4:8], lhsT=ONES, rhs=gg[:, t, :], start=True, stop=True)
                for h in H4:
                    P.I("dve", "tensor_scalar", reads=["cst", "b_g"], writes=[K("gU", h)], out=gU[:, h, :], in0=UM,
                        scalar1=gg[:, t, h:h + 1], scalar2=None, op0=ALU.mult)
                yield
                P.I("dve", "tensor_copy", reads=[kGl], writes=[K("Gt")], out=Gt[:, :], in_=pG[:, :])
                done(kGl)
                P.I("dve", "tensor_tensor", reads=[K("Gt")], writes=[K("sc", 1)], out=sc[:, 4:8], in0=Gt[:, 4:8], in1=Gt[:, 0:4], op=ALU.subtract)
                pGr, kGr = nxt()
                P.I("pe", "matmul", reads=[K("gU", h) for h in H4] + ["cst"], writes=[kGr], out=pGr[:, :, :], lhsT=ONES,
                    rhs=gU[:, :, :], start=True, stop=True)
                yield
                P.I("act", "activation", reads=[K("sc", 1)], writes=[K("sc", 1)], out=sc[:, 4:8], in_=sc[:, 4:8], func=AF.Exp)
                P.I("act", "activation", reads=[K("Gt")], writes=[K("sc", 2)], out=sc[:, 8:12], in_=Gt[:, 4:8], func=AF.Exp)
                P.I("act", "activation", reads=[K("Gt")], writes=[K("sc", 3)], out=sc[:, 12:16], in_=Gt[:, 0:4], func=AF.Exp)
                P.I("act", "activation", reads=[kGr], writes=[K("eGr")], out=eGr[:, :, :], in_=pGr[:, :, :], func=AF.Exp)
                for h in H4:
                    P.I("dve", "scalar_tensor_tensor", reads=[kGr, K("Gt"), "b_pmsl4"], writes=[K("tA", h)], out=tA[:, h, :],
                        in0=pGr[:, h, :], scalar=Gt[:, h:h + 1], in1=pmsl4[:, h, :], op0=ALU.subtract, op1=ALU.add)
                    P.I("dve", "scalar_tensor_tensor", reads=[kGr, K("Gt"), "b_mui4"], writes=[K("tAT", h)], out=tAT[:, h, :],
                        in0=pGr[:, h, :], scalar=Gt[:, h:h + 1], in1=mui4[:, h, :], op0=ALU.subtract, op1=ALU.add)
                done(kGr)
                yield
                P.I("dve", "tensor_tensor", reads=[K("sc", 3), "b_nbeta"], writes=[K("sc", 0)], out=sc[:, 0:4], in0=sc[:, 12:16],
                    in1=nbeta[:, t, :], op=ALU.mult)
                P.I("act", "activation", reads=[K("tA", h) for h in H4], writes=[K("Dm")], out=Dm[:, :, :], in_=tA[:, :, :], func=AF.Exp, scale=-1.0)
                P.I("act", "activation", reads=[K("tAT", h) for h in H4], writes=[K("DT")], out=DT[:, :, :], in_=tAT[:, :, :], func=AF.Exp)
                pKK, kKK = nxt()
                pQK, kQK = nxt()
                for h in H4:
                    P.I("pe", "matmul", reads=[QK[4 + h]], writes=[kKK], out=pKK[:, h, :], lhsT=qkb[:, 4 + h, cs], rhs=qkb[:, 4 + h, cs], start=True, stop=True)
                    P.I("pe", "matmul", reads=[QK[4 + h], QK[h]], writes=[kQK], out=pQK[:, h, :], lhsT=qkb[:, 4 + h, cs], rhs=qkb[:, h, cs], start=True, stop=True)
                yield
                x0 = X[0]
                for h in H4:
                    P.I("dve", "scalar_tensor_tensor", reads=[kKK, "b_nbeta", K("Dm")], writes=[K("X", 0, h)], out=x0[:, h, :],
                        in0=pKK[:, h, :], scalar=nbeta[:, t, h:h + 1], in1=Dm[:, h, :], op0=ALU.mult, op1=ALU.mult)
                done(kKK)
                P.I("dve", "tensor_tensor", reads=[kQK, K("DT")], writes=[K("AqkT")], out=AqkT[:, :, :], in0=pQK[:, :, :], in1=DT[:, :, :], op=ALU.mult)
                done(kQK)
                IDK = idb[:, :] if DK == BF16 else IDF
                pTk, kTk = nxt()
                vTk = bfv(pTk, DK)
                for h in H4:
                    P.I("pe", "transpose", reads=[QK[4 + h], "idb", "cst"], writes=[kTk], out=vTk[:, h, 0:128], in_=qkb[:, 4 + h, cs], identity=IDK)
                yield
                pXT, kXT = nxt()
                for h in H4:
                    P.I("pe", "transpose", reads=[K("X", 0, h), "cst"], writes=[kXT], out=pXT[:, h, :], in_=x0[:, h, :], identity=IDF)
                for h in H4:
                    P.I("dve", "tensor_scalar", reads=[kTk, K("sc", 0)], writes=[K("kbgn", h)], out=kbgn[:, h, :], in0=vTk[:, h, 0:128],
                        scalar1=sc[:, h:h + 1], scalar2=None, op0=ALU.mult)
                    P.I("dve", "tensor_scalar", reads=[kTk, K("sc", 1)], writes=[K("kg", h)], out=kg[:, h, :], in0=vTk[:, h, 0:128],
                        scalar1=sc[:, 4 + h:5 + h], scalar2=None, op0=ALU.mult)
                done(kTk)
                yield
                X0K = [K("X", 0, h) for h in H4]
                P.I("act", "copy", reads=[kXT], writes=[K("XT", 0)], out=XT[0][:, :, :], in_=pXT[:, :, :])
                P.I("dve", "tensor_tensor", reads=[kXT, "b_id4"], writes=[K("Pt", 0)], out=Pt[0][:, :, :], in0=pXT[:, :, :], in1=id4[:, :, :], op=ALU.add)
                done(kXT)
                pTv, kTv = nxt()
                vTv = bfv(pTv, DK)
                for h in H4:
                    P.I("pe", "transpose", reads=[QK[8 + h], "idb", "cst"], writes=[kTv], out=vTv[:, h, 0:128], in_=qkb[:, 8 + h, cs], identity=IDK)
                yield
                for h in H4:
                    P.I("dve", "tensor_scalar", reads=[kTv, "b_beta"], writes=[K("vb", h)], out=vb[:, h, :], in0=vTv[:, h, 0:128],
                        scalar1=beta[:, t, h:h + 1], scalar2=None, op0=ALU.mult)
                done(kTv)
                cur = 0
                pcur = 0
                xkeys = X0K
                for lv in range(1, 7):
                    nx = 1 - cur
                    pX2, kX2 = nxt()
                    for h in H4:
                        P.I("pe", "matmul", reads=xkeys + [K("XT", cur)], writes=[kX2], out=pX2[:, h, :], lhsT=XT[cur][:, h, :], rhs=X[cur][:, h, :], start=True, stop=True)
                    if lv < 6:
                        pXT2, kXT2 = nxt()
                        for h in H4:
                            P.I("pe", "matmul", reads=xkeys + [K("XT", cur)], writes=[kXT2], out=pXT2[:, h, :], lhsT=X[cur][:, h, :], rhs=XT[cur][:, h, :], start=True, stop=True)
                    yield
                    P.I("dve", "tensor_tensor", reads=[kX2, "b_id4"], writes=[K("IX")], out=IX[:, :, :], in0=pX2[:, :, :], in1=id4[:, :, :], op=ALU.add)
                    if lv < 6:
                        P.I("act", "copy", reads=[kX2], writes=[K("X", nx, h) for h in H4], out=X[nx][:, :, :], in_=pX2[:, :, :])
                        P.I("act", "copy", reads=[kXT2], writes=[K("XT", nx)], out=XT[nx][:, :, :], in_=pXT2[:, :, :])
                        done(kXT2)
                    done(kX2)
                    yield
                    pP, kP = nxt()
                    for h in H4:
                        P.I("pe", "matmul", reads=[K("IX"), K("Pt", pcur)], writes=[kP], out=pP[:, h, :], lhsT=IX[:, h, :], rhs=Pt[pcur][:, h, :], start=True, stop=True)
                    yield
                    if lv < 6:
                        P.I("dve", "tensor_copy", reads=[kP], writes=[K("Pt", 1 - pcur)], out=Pt[1 - pcur][:, :, :], in_=pP[:, :, :])
                    else:
                        P.I("dve", "tensor_copy", reads=[kP], writes=[K("Ptf")], out=Ptf[:, :, :], in_=pP[:, :, :])
                    done(kP)
                    pcur = 1 - pcur
                    cur = nx
                    xkeys = [K("X", cur, h) for h in H4]
                    yield
                PT_ = Ptf
                kPt = K("Ptf")
                pW, kW = nxt()
                for h in H4:
                    P.I("pe", "matmul", reads=[K("kbgn", h), kPt], writes=[kW], out=pW[:, h, :], lhsT=kbgn[:, h, :], rhs=PT_[:, h, :], start=True, stop=True)
                P.I("pool", "tensor_tensor", reads=QK[0:4] + [K("eGr")], writes=[K("qgT")], out=qgT[:, :, :], in0=qkb[:, 0:4, cs], in1=eGr[:, :, :], op=ALU.mult)
                yield
                P.I("act", "copy", reads=[kW], writes=[K("WTn")], out=WTn[:, :, :], in_=pW[:, :, :])
                done(kW)
                yield

            def chunk_scan(j, ho, t0=t0, s=s):
                t = t0 + j
                hn = ho["n"]
                cs = slice(j * 128, (j + 1) * 128)
                sz = szs[s % 3]
                sc, AqkT, kg, vb, WTn, qgT, PT_ = (ho[k] for k in ("sc", "AqkT", "kg", "vb", "WTn", "qgT", "Ptf"))
                K = lambda *a: ("b_ho", hn) + a
                kPt = K("Ptf")
                pV, kV = nxt()
                for h in H4:
                    P.I("pe", "matmul", reads=[kPt, K("vb", h)], writes=[kV], out=pV[:, h, :], lhsT=PT_[:, h, :], rhs=vb[:, h, :], start=True, stop=False)
                    P.I("pe", "matmul", reads=[K("WTn"), ("b_S", h)], writes=[kV], out=pV[:, h, :], lhsT=WTn[:, h, :], rhs=St[:, h, :], start=False, stop=True)
                yield
                P.I("dve", "tensor_copy", reads=[kV], writes=["b_vnew"], out=vnew[:, :, :], in_=pV[:, :, :])
                done(kV)
                yield
                pO, kO = nxt()
                for h in H4:
                    P.I("pe", "matmul", reads=["b_Sb", K("qgT")], writes=[kO], out=pO[:, h, :], lhsT=Sb[:, h, :], rhs=qgT[:, h, :], start=True, stop=False)
                    P.I("pe", "matmul", reads=["b_vnew", K("AqkT")], writes=[kO], out=pO[:, h, :], lhsT=vnew[:, h, :], rhs=AqkT[:, h, :], start=False, stop=True)
                pSn, kSn = nxt()
                for h in H4:
                    P.I("pe", "matmul", reads=[K("kg", h), "b_vnew"], writes=[kSn], out=pSn[:, h, :], lhsT=kg[:, h, :], rhs=vnew[:, h, :], start=True, stop=True)
                yield
                for h in H4:
                    P.I("dve", "scalar_tensor_tensor", reads=[("b_S", h), K("sc", 2), kSn], writes=[("b_S", h)], out=St[:, h, :],
                        in0=St[:, h, :], scalar=sc[:, 8 + h:9 + h], in1=pSn[:, h, :], op0=ALU.mult, op1=ALU.add)
                done(kSn)
                P.I("act", "copy", reads=[("b_S", h) for h in H4], writes=["b_Sb"], out=Sb[:, :, :], in_=St[:, :, :])
                yield
                P.I("act", "activation", reads=[kO], writes=["b_osq"], out=osq[:, :, :], in_=pO[:, :, :], func=AF.Square)
                yield
                pSS, kSS = nxt()
                P.I("pe", "matmul", reads=["oneb", "b_osq"], writes=[kSS], out=pSS[:, :, :], lhsT=oneb[:, :], rhs=osq[:, :, :], start=True, stop=True)
                yield
                P.I("act", "activation", reads=[kSS, "epsc"], writes=["b_rno"], out=rno[:, :, :], in_=pSS[:, :, :], func=AF.Ln,
                    bias=epsc[:, 0:1], scale=1.0 / 128)
                P.I("act", "activation", reads=["b_rno"], writes=["b_rno"], out=rno[:, :, :], in_=rno[:, :, :], func=AF.Exp, scale=-0.5)
                done(kSS)
                yield
                P.I("dve", "scalar_tensor_tensor", reads=[kO, "prm", "b_rno"], writes=["b_t1"], out=t1[:, :, :], in0=pO[:, :, :],
                    scalar=prm[:, P_GDN:P_GDN + 1], in1=rno[:, :, :], op0=ALU.mult, op1=ALU.mult)
                done(kO)
                obb = ob[t % 2]
                P.I("pool", "tensor_tensor", reads=["b_t1", ("b_sz", s % 3)], writes=[("b_ob", t % 2)], out=obb[:, :, :], in0=t1[:, :, :],
                    in1=sz[:, :, cs], op=ALU.mult)
                P.dma(env["oT"].ap()[4:8, :, t * 128:(t + 1) * 128].rearrange("k p c -> p k c"), obb[:, :, :],
                      reads=[("b_ob", t % 2)], writes=[("oT_dn", t)], sem=f"b_so{t % 2}", eng="pool")
                yield

            for j0 in range(0, ntile, 2):
                js = [j for j in (j0, j0 + 1) if j < ntile]
                pp_ = state_b["pair"] % 2
                state_b["pair"] += 1
                hos = [HO[2 * pp_ + i] for i in range(len(js))]

                def scan_chain(js=js, hos=hos, scan_fn=chunk_scan):
                    for i, j in enumerate(js):
                        yield from scan_fn(j, hos[i])

                gens = [chunk_prep(j, BS[i], hos[i]) for i, j in enumerate(js)] + nextconv
                nextconv = []
                if state_b["prev"] is not None:
                    gens.append(state_b["prev"])
                drive(gens)
                state_b["prev"] = scan_chain()
        drive([state_b["prev"]])


def phase_c(env):
    nc, P, NT, T = env["nc"], env["P"], env["NT"], env["T"]
    prm, cst, oneb, epsc = env["prm"], env["cst"], env["oneb"], env["epsc"]
    SCALE = 64 ** -0.5
    LAM_INIT = 0.8 - 0.6 * math.exp(-0.3 * 0)
    with ExitStack() as st:
        S = lambda name, shape, dt: st.enter_context(nc.sbuf_tensor(name, shape, dt))
        PS = lambda name, shape, dt: st.enter_context(nc.psum_tensor(name, shape, dt))
        relb_t = S("c_relb", [32, 4], F32)
        gv_t = S("c_gv", [4, 512], F32)
        lt = S("c_lt", [128, 128], F32)
        lsum = S("c_lsum", [128, 2], F32)
        neglam = S("c_neglam", [128, 1], F32)
        gsub = S("c_gsub", [128, 1], F32)
        qT = [[S(f"c_qT{p}{i}", [128, T], BF16) for i in range(2)] for p in range(2)]
        kT = [S(f"c_kT{p}", [128, T], BF16) for p in range(2)]
        vh = [S(f"c_vh{p}", [128, NT, 128], BF16) for p in range(2)]
        vmeta = [S(f"c_vmeta{p}", [16, 128], BF16) for p in range(2)]
        hk0 = [S(f"c_hk0{p}", [128, 128], F32) for p in range(2)]
        hk1 = [S(f"c_hk1{p}", [128, 128], F32) for p in range(2)]
        hkm = [S(f"c_hkm{p}", [16, 128], F32) for p in range(2)]
        B0 = [S(f"c_B0{p}", [128, 128], F32) for p in range(2)]
        B1 = [S(f"c_B1{p}", [128, 128], F32) for p in range(2)]
        Bm = [S(f"c_Bm{p}", [16, 128], F32) for p in range(2)]
        PT = [S(f"c_PT{i}", [128, 512], BF16) for i in range(4)]
        tmpS = [S(f"c_tmpS{i}", [128, 128], F32) for i in range(2)]
        oc = [[S(f"c_oc{p}{i}", [128, 512], F32) for i in range(2)] for p in range(2)]
        rz = [[S(f"c_rz{p}{i}", [128, 512], F32) for i in range(2)] for p in range(2)]
        osq = [S(f"c_osq{i}", [128, 512], F32) for i in range(2)]
        rstd = [S(f"c_rstd{i}", [128, 512], F32) for i in range(2)]
        onb = [S(f"c_onb{i}", [128, 512], BF16) for i in range(2)]
        zpad = S("c_zpad", [128, 112], BF16)
        mpo = [S(f"c_mpo{i}", [128, 16], F32) for i in range(2)]
        mpz = [S(f"c_mpz{i}", [128, 16], F32) for i in range(2)]
        Pacc = [S(f"c_Pacc{i}", [128, 512], F32) for i in range(2)]
        pS = [PS(f"c_pS{i}", [128, 512], F32) for i in range(3)]
        pO = [PS(f"c_pO{i}", [128, 512], F32) for i in range(2)]
        pZ = [PS(f"c_pZ{i}", [128, 512], F32) for i in range(2)]
        pN = PS("c_pN", [128, 512], F32)
        pM = pN

        for i in range(2):
            a0 = P_LAM + 128 * i
            P.I("dve", "tensor_tensor", reads=["prm"], writes=["c_lt"], out=lt[:, 0:64], in0=prm[:, a0:a0 + 64],
                in1=prm[:, a0 + 64:a0 + 128], op=ALU.mult)
            P.I("dve", "reduce_sum", reads=["c_lt"], writes=[("c_lsum", i)], out=lsum[:, i:i + 1], in_=lt[:, 0:64],
                axis=mybir.AxisListType.X)
        P.I("act", "activation", reads=[("c_lsum", 0), ("c_lsum", 1)], writes=[("c_lsum", 0), ("c_lsum", 1)],
            out=lsum[:, :], in_=lsum[:, :], func=AF.Exp)
        P.I("dve", "scalar_tensor_tensor", reads=[("c_lsum", 0), ("c_lsum", 1)], writes=["c_neglam"],
            out=neglam[:, :], in0=lsum[:, 1:2], scalar=-LAM_INIT, in1=lsum[:, 0:1], op0=ALU.add, op1=ALU.subtract)
        P.I("dve", "tensor_scalar", reads=["prm"], writes=["c_gsub"], out=gsub[:, :], in0=prm[:, P_GSUB:P_GSUB + 1],
            scalar1=1.0 - LAM_INIT, scalar2=None, op0=ALU.mult)
        P.I("pool", "memset", writes=["c_zpad"], ap=zpad[:, :], constant=0.0)
        P.dma(relb_t[:, :], env["relb"].ap()[:, :], writes=["c_relb"], sem="l_relb")
        P.I("pe", "matmul", reads=["c_relb", "cst"], writes=["c_pN"], out=pN[0:4, :], lhsT=relb_t[:, :],
            rhs=cst[0:32, C_E1:C_E1 + 512], start=True, stop=True)
        P.I("dve", "tensor_copy", reads=["c_pN"], writes=["c_gv"], out=gv_t[:, :], in_=pN[0:4, :])
        P.dma(env["fvec"].ap()[:, :], gv_t[:, :], reads=["c_gv"], writes=["fvec"], sem="s_fvec")

        nq_tiles = NT - 1
        nsup = (nq_tiles + 3) // 4

        def head_loads(h):
            p = h % 2
            P.dma(qT[p][0][:, :], env["qT_da"].ap()[h, :, :], writes=[("c_qT", p, 0)], sem=f"c_l0{p}")
            P.dma(qT[p][1][:, :], env["qT_da"].ap()[h, :, :], writes=[("c_qT", p, 1)], sem=f"c_l7{p}")
            P.I("pool", "memset", writes=[("c_qT", p, 0)], ap=qT[p][0][64:128, :], constant=0.0)
            P.I("pool", "memset", writes=[("c_qT", p, 1)], ap=qT[p][1][0:64, :], constant=0.0)
            P.dma(kT[p][:, :], env["kT_da"].ap()[h, :, :], writes=[("c_kT", p)], sem=f"c_l1{p}")
            P.dma(vh[p][:, :, :], env["v_da"].ap()[:, h * 128:(h + 1) * 128].rearrange("(t p) d -> p t d", p=128),
                  writes=[("c_vh", p)], sem=f"c_l2{p}")
            P.dma(vmeta[p][:, :], env["v_da"].ap()[112:128, h * 128:(h + 1) * 128], writes=[("c_vmeta", p)], sem=f"c_l3{p}")
            P.dma(hk0[p][:, :], bass.AP(env["fvec"], h * 512 + 129, [[1, 128], [1, 128]]),
                  reads=["fvec"], writes=[("c_hk0", p)], sem=f"c_l4{p}")
            P.dma(hk1[p][:, :], bass.AP(env["fvec"], h * 512 + 257, [[1, 128], [1, 128]]),
                  reads=["fvec"], writes=[("c_hk1", p)], sem=f"c_l5{p}")
            P.dma(hkm[p][:, :], bass.AP(env["fvec"], h * 512 + 257, [[1, 16], [1, 128]]),
                  reads=["fvec"], writes=[("c_hkm", p)], sem=f"c_l6{p}")

        def head_bias(h, staged=True):
            p = h % 2
            c15h = prm[:, P_C15 + h:P_C15 + h + 1]

            def s0():
                P.I("pe", "matmul", reads=["cst", ("c_hk0", p)], writes=["c_pN"], out=pM[:, 0:128], lhsT=cst[:, C_J:C_J + 128],
                    rhs=hk0[p][:, :], start=True, stop=True)

            def s1():
                P.I("dve", "tensor_scalar", reads=["c_pN", "prm"], writes=[("c_B0", p)], out=B0[p][:, :], in0=pM[:, 0:128],
                    scalar1=c15h, scalar2=1.0 / SCALE, op0=ALU.subtract, op1=ALU.mult)
                P.I("dve", "memset", writes=[("c_B0", p)], ap=B0[p][64:128, 0:64], constant=-30000.0 / SCALE)

            def s2():
                P.I("pe", "matmul", reads=["cst", ("c_hk1", p)], writes=["c_pN"], out=pM[:, 0:128], lhsT=cst[:, C_J:C_J + 128],
                    rhs=hk1[p][:, :], start=True, stop=True)

            def s3():
                P.I("dve", "tensor_scalar", reads=["c_pN", "prm"], writes=[("c_B1", p)], out=B1[p][:, :], in0=pM[:, 0:128],
                    scalar1=c15h, scalar2=1.0 / SCALE, op0=ALU.subtract, op1=ALU.mult)

            def s4():
                P.I("pe", "matmul", reads=["cst", ("c_hkm", p)], writes=["c_pN"], out=pM[0:16, 0:128],
                    lhsT=cst[0:16, C_J + 112:C_J + 128], rhs=hkm[p][:, :], start=True, stop=True)

            def s5():
                P.I("dve", "tensor_scalar", reads=["c_pN", "prm"], writes=[("c_Bm", p)], out=Bm[p][:, :], in0=pM[0:16, 0:128],
                    scalar1=c15h[0:16, :], scalar2=1.0 / SCALE, op0=ALU.subtract, op1=ALU.mult)

            def s01():
                s0()
                s1()

            def s23():
                s2()
                s3()

            def s45():
                s4()
                s5()
                bias_done.add(h)

            stages = [s01, s23, s45]
            if not staged:
                for f in stages:
                    f()
            else:
                for i, f in enumerate(stages):
                    defer(2 + 3 * i, f)

        pending = []

        def defer(delay, fn):
            pending.append([delay, fn])

        def tick():
            due = []
            for ent in pending:
                ent[0] -= 1
                if ent[0] <= 0:
                    due.append(ent)
            for ent in due:
                pending.remove(ent)
            for ent in due:
                ent[1]()

        def flush_pending():
            while pending:
                tick()

        state = dict(pti=0, acc=0)
        norm_done = [True, True]
        bias_done = set()
        head_loads(0)
        head_bias(0, staged=False)
        bias_done.add(0)

        comps = []
        blkc = 0
        for h in range(4):
            hp = h % 2
            KT = kT[hp]
            VH = vh[hp]
            VM = vmeta[hp]
            RK = [("c_kT", hp), ("c_vh", hp), ("c_vmeta", hp), ("c_B0", hp), ("c_B1", hp), ("c_Bm", hp)]
            blocks = []
            blocks.append((112, 16, [(112, 16, VM[0:16, :], [(0, 16, "near", B0[hp][0:16, 0:16])])]))
            for qs in range(nsup):
                qt0 = 1 + 4 * qs
                nq = min(4, NT - qt0)
                Wq = nq * 128
                items = []
                segs = []
                if qs == 0:
                    segs.append((0, 128, "near", Bm[hp][0:16, :]))
                    if Wq > 128:
                        segs.append((128, Wq, "far", None))
                else:
                    segs.append((0, Wq, "far", None))
                items.append((112, 16, VM[0:16, :], segs))
                for kt in range(1, qt0 + nq):
                    jmin = max(0, kt - qt0)
                    segs = []
                    j = jmin
                    if kt == qt0 + j:
                        segs.append((128 * j, 128 * j + 128, "near", B0[hp][:, :]))
                        j += 1
                    if j < nq and kt == qt0 + j - 1:
                        segs.append((128 * j, 128 * j + 128, "near", B1[hp][:, :]))
                        j += 1
                    if j < nq:
                        segs.append((128 * j, Wq, "far", None))
                    items.append((kt * 128, 128, VH[:, kt, :], segs))
                blocks.append((qt0 * 128, Wq, items))
            for bi, (qc0, Wq, items) in enumerate(blocks):
                for c in range(2):
                    comps.append(dict(h=h, hp=hp, bi=bi, nblocks=len(blocks), c=c, qc0=qc0, Wq=Wq, items=items,
                                      KT=KT, RK=RK, bp=blkc % 2, c15=prm[:, P_C15 + h:P_C15 + h + 1]))
                blkc += 1
        flat = [(ci, i) for ci, cm in enumerate(comps) for i in range(len(cm["items"]))]

        def rec_S(f):
            ci, i = flat[f]
            cm = comps[ci]
            if cm["h"] not in bias_done:
                flush_pending()
            kc0, M, vl, segs = cm["items"][i]
            lo = segs[0][0]
            b = f % 3
            hp, c, Wq, qc0 = cm["hp"], cm["c"], cm["Wq"], cm["qc0"]
            nears = [sg for sg in segs if sg[2] == "near"]
            P.I("pe", "matmul", reads=[("c_kT", hp), ("c_qT", hp, c)], writes=[("c_pS", b)],
                out=pS[b][0:M, lo:Wq], lhsT=cm["KT"][:, kc0:kc0 + M],
                rhs=qT[hp][c][:, qc0 + lo:qc0 + Wq], start=True, stop=(len(nears) == 0))
            for ni, (c0, c1, kind, bap) in enumerate(nears):
                P.I("pe", "matmul", reads=["cst"] + cm["RK"], writes=[("c_pS", b)],
                    out=pS[b][0:M, c0:c1], lhsT=cst[0:M, C_ID:C_ID + M], rhs=bap, start=False, stop=(ni == len(nears) - 1))

        def chain(c, bp, Wq, PO, PZ, kPO, kPZ, qc0, h, ai, PA, kPA, n):
            RZ = rz[bp][c]
            krz = ("c_rz", bp, c)
            O0 = oc[bp][0]

            def n_z():
                P.I("pe", "matmul", reads=[kPA, "cst"], writes=[kPZ],
                    out=PZ[:, 0:Wq], lhsT=cst[:, C_ONE:C_ONE + 128], rhs=PA[:, 0:Wq], start=False, stop=True)
                defer(2, n_a)

            def n_a():
                P.I("act", "activation", reads=[kPZ], writes=[krz], out=RZ[:, 0:Wq], in_=PZ[:, 0:Wq], func=AF.Ln)
                defer(1, n_b)

            def n_b():
                P.I("act", "activation", reads=[krz], writes=[krz], out=RZ[:, 0:Wq], in_=RZ[:, 0:Wq], func=AF.Exp, scale=-1.0)
                P.I("dve", "tensor_tensor", reads=[kPO, krz], writes=[("c_oc", bp, c)],
                    out=oc[bp][c][:, 0:Wq], in0=PO[:, 0:Wq], in1=RZ[:, 0:Wq], op=ALU.mult)
                if ai is not None:
                    norm_done[ai] = True
                if c == 1:
                    defer(1, e_a)

            def e_a():
                P.I("dve", "scalar_tensor_tensor", reads=[("c_oc", bp, 0), ("c_oc", bp, 1), "c_neglam"], writes=[("c_oc", bp, 0)],
                    out=O0[:, 0:Wq], in0=oc[bp][1][:, 0:Wq], scalar=neglam[:, 0:1], in1=O0[:, 0:Wq], op0=ALU.mult, op1=ALU.add)
                P.I("dve", "tensor_tensor", reads=[("c_oc", bp, 0)], writes=[("c_osq", bp)],
                    out=osq[bp][:, 0:Wq], in0=O0[:, 0:Wq], in1=O0[:, 0:Wq], op=ALU.mult)
                defer(2, e_b)

            def e_b():
                P.I("pe", "matmul", reads=["cst", ("c_osq", bp)], writes=["c_pN"],
                    out=pN[:, 0:Wq], lhsT=cst[:, C_ONE:C_ONE + 128], rhs=osq[bp][:, 0:Wq], start=True, stop=True)
                P.I("act", "activation", reads=["c_pN", "epsc"], writes=[("c_rstd", bp)],
                    out=rstd[bp][:, 0:Wq], in_=pN[:, 0:Wq], func=AF.Ln, bias=epsc[:, 0:1], scale=1.0 / 128)
                defer(1, e_d)

            def e_d():
                P.I("act", "activation", reads=[("c_rstd", bp)], writes=[("c_rstd", bp)],
                    out=rstd[bp][:, 0:Wq], in_=rstd[bp][:, 0:Wq], func=AF.Exp, scale=-0.5)
                P.I("dve", "scalar_tensor_tensor", reads=[("c_oc", bp, 0), "c_gsub", ("c_rstd", bp)], writes=[("c_onb", bp)],
                    out=onb[bp][:, 0:Wq], in0=O0[:, 0:Wq], scalar=gsub[:, 0:1], in1=rstd[bp][:, 0:Wq], op0=ALU.mult, op1=ALU.mult)
                P.dma(env["oT"].ap()[h, :, qc0:qc0 + Wq], onb[bp][:, 0:Wq], reads=[("c_onb", bp)],
                      writes=[("oT", h, qc0)], sem=f"c_st{bp}")

            if n > 1:
                defer(1, n_z)
            else:
                defer(2, n_a)


        rec_S(0)
        if len(flat) > 1:
            rec_S(1)
        ctx = {}
        for f, (ci, i) in enumerate(flat):
            cm = comps[ci]
            h, hp, bi, c, qc0, Wq, items, RK, bp, c15 = (cm[k] for k in ("h", "hp", "bi", "c", "qc0", "Wq", "items", "RK", "bp", "c15"))
            n = len(items)
            if i == 0:
                if c == 0:
                    nb = cm["nblocks"]
                    if h == 0 and bi == min(3, nb - 1):
                        load_weight_bf16(P, nc, env["wup"], env["w_up"], 8, 2 * D_FF, "e_wup", "wE")
                    if bi == min(2, nb - 1) and h + 1 < 4:
                        head_loads(h + 1)
                    if bi == min(4, nb - 1) and h + 1 < 4:
                        head_bias(h + 1)
                if n == 1:
                    bb = f % 3
                    ctx = dict(ai=None, PO=pS[bb][:, 256:272], PZ=pS[bb][:, 288:304], kPO=("c_pS", bb), kPZ=("c_pS", bb),
                               PA=Pacc[0], kPA=("c_Pacc", 0))
                else:
                    ai = state["acc"] % 2
                    state["acc"] += 1
                    if not norm_done[ai]:
                        flush_pending()
                    ctx = dict(ai=ai, PO=pO[ai], PZ=pZ[ai], kPO=("c_pO", ai), kPZ=("c_pZ", ai), PA=Pacc[ai], kPA=("c_Pacc", ai))
            ai, PO, PZ, kPO, kPZ, PA, kPA = (ctx[k] for k in ("ai", "PO", "PZ", "kPO", "kPZ", "PA", "kPA"))
            kc0, M, vl, segs = items[i]
            lo = segs[0][0]
            b = f % 3
            if f + 2 < len(flat):
                rec_S(f + 2)
            pt = PT[state["pti"] % 4]
            ptk = ("c_PT", state["pti"] % 4)
            state["pti"] += 1
            P.I("act", "activation", reads=[("c_pS", b), "prm"], writes=[ptk],
                out=pt[0:M, lo:Wq], in_=pS[b][0:M, lo:Wq], func=AF.Exp, bias=c15[0:M, :], scale=SCALE)
            P.I("pe", "matmul", reads=[ptk] + RK, writes=[kPO],
                out=PO[:, lo:Wq], lhsT=vl, rhs=pt[0:M, lo:Wq], start=(i == 0), stop=(i == n - 1))
            if i == 0:
                P.I("pe", "matmul", reads=[ptk, "oneb"], writes=[kPZ],
                    out=PZ[:, lo:Wq], lhsT=oneb[0:M, :], rhs=pt[0:M, lo:Wq], start=True, stop=(n == 1))
            elif i == 1:
                P.I("dve", "tensor_copy", reads=[ptk], writes=[kPA], out=PA[:, lo:Wq], in_=pt[:, lo:Wq])
            else:
                P.I("dve", "tensor_tensor", reads=[ptk, kPA], writes=[kPA], out=PA[:, lo:Wq], in0=PA[:, lo:Wq],
                    in1=pt[:, lo:Wq], op=ALU.add)
            tick()
            if i == n - 1:
                if n == 1:
                    P.I("dve", "tensor_copy", reads=[kPO], writes=[("c_mpo", c)], out=mpo[c][:, 0:Wq], in_=PO[:, 0:Wq])
                    P.I("dve", "tensor_copy", reads=[kPZ], writes=[("c_mpz", c)], out=mpz[c][:, 0:Wq], in_=PZ[:, 0:Wq])
                    chain(c, bp, Wq, mpo[c], mpz[c], ("c_mpo", c), ("c_mpz", c), qc0, h, None, PA, kPA, n)
                else:
                    norm_done[ai] = False
                    chain(c, bp, Wq, PO, PZ, kPO, kPZ, qc0, h, ai, PA, kPA, n)
                if c == 1 and bi == cm["nblocks"] - 1:
                    P.dma(env["oT"].ap()[h, :, 0:112], zpad[:, :], reads=["c_zpad"], writes=[("oT", h, 0)], sem="c_st2")
        flush_pending()


def phase_d(env):
    nc, P, NT, T = env["nc"], env["P"], env["NT"], env["T"]
    idb, epsc = env["idb"], env["epsc"]
    with ExitStack() as st:
        S = lambda name, shape, dt: st.enter_context(nc.sbuf_tensor(name, shape, dt))
        PS = lambda name, shape, dt: st.enter_context(nc.psum_tensor(name, shape, dt))
        wout = S("d_wout", [128, 8, D], BF16)
        g2 = S("d_g2", [128, D], F32)
        ot = [S(f"d_ot{i}", [128, 8, 128], BF16) for i in range(2)]
        ht = [S(f"d_h{i}", [128, D], F32) for i in range(2)]
        hm = [S(f"d_hm{i}", [128, D], F32) for i in range(2)]
        junk = S("d_junk", [128, D], BF16)
        ss = [S(f"d_ss{i}", [128, 1], F32) for i in range(2)]
        rstd = [S(f"d_rs{i}", [128, 1], F32) for i in range(2)]
        ub = [S(f"d_u{i}", [128, D], BF16) for i in range(2)]
        ut = [S(f"d_ut{i}", [128, 8, 128], BF16) for i in range(2)]
        pm = [PS(f"d_pm{i}", [128, 512], F32) for i in range(4)]
        pT = [PS(f"d_pT{i}", [128, 8, 128], BF16) for i in range(2)]
        P.dma(g2[:, :], env["gb"].ap()[:, D:2 * D], writes=["d_g2"], sem="l_g2")
        load_weight_bf16(P, nc, wout, env["w_out"], 8, D, "d_wout", "wD")
        WK = ["d_wout"] * 8

        def load(t):
            b = t % 2
            P.dma(ot[b][:, :, :], env["oT"].ap()[:, :, t * 128:(t + 1) * 128].rearrange("k p c -> p k c"),
                  writes=[("d_ot", b)], sem=f"d_lo{b}")
            P.dma(ht[b][:, :], env["hin"].ap()[t * 128:(t + 1) * 128, :], writes=[("d_h", b)], sem=f"d_lh{b}")

        def mm(t):
            b = t % 2
            for half in range(2):
                pb = (2 * t + half) % 4
                for kc in range(8):
                    P.I("pe", "matmul", reads=[("d_ot", b), WK[kc]], writes=[("d_pm", pb)],
                        out=pm[pb][:, :], lhsT=ot[b][:, kc, :], rhs=wout[:, kc, half * 512:(half + 1) * 512],
                        start=(kc == 0), stop=(kc == 7))

        load(0)
        if NT > 1:
            load(1)
        mm(0)
        for t in range(NT):
            b = t % 2
            if t + 1 < NT:
                mm(t + 1)
            for half in range(2):
                pb = (2 * t + half) % 4
                P.I("dve", "tensor_tensor", reads=[("d_pm", pb), ("d_h", b)], writes=[("d_hm", b, half)],
                    out=hm[b][:, half * 512:(half + 1) * 512], in0=pm[pb][:, :],
                    in1=ht[b][:, half * 512:(half + 1) * 512], op=ALU.add)
            if t + 2 < NT:
                load(t + 2)
            HM = [("d_hm", b, 0), ("d_hm", b, 1)]
            P.dma(env["hmid"].ap()[t * 128:(t + 1) * 128, :], hm[b][:, :], reads=HM, writes=[("hmid", t)], sem=f"d_sh{b}", eng="pool")
            P.I("act", "activation", reads=HM, writes=["d_junk", ("d_ss", b)],
                out=junk[:, :], in_=hm[b][:, :], func=AF.Square, accum_out=ss[b][:, :])
            P.I("act", "activation", reads=[("d_ss", b), "epsc"], writes=[("d_rs", b)],
                out=rstd[b][:, :], in_=ss[b][:, :], func=AF.Sqrt, bias=epsc[:, 0:1], scale=1.0 / D)
            P.I("dve", "reciprocal", reads=[("d_rs", b)], writes=[("d_rs", b)], out=rstd[b][:, :], in_=rstd[b][:, :])
            P.I("dve", "scalar_tensor_tensor", reads=HM + [("d_rs", b), "d_g2"], writes=[("d_u", b)],
                out=ub[b][:, :], in0=hm[b][:, :], scalar=rstd[b][:, 0:1], in1=g2[:, :], op0=ALU.mult, op1=ALU.mult)
            for kc in range(8):
                P.I("pe", "transpose", reads=[("d_u", b), "idb"], writes=[("d_pT", b)],
                    out=pT[b][:, kc, :], in_=ub[b][:, kc * 128:(kc + 1) * 128], identity=idb[:, :])
            P.I("act", "copy", reads=[("d_pT", b)], writes=[("d_ut", b)], out=ut[b][:, :, :], in_=pT[b][:, :, :])
            P.dma(env["u2T"].ap()[:, :, t * 128:(t + 1) * 128].rearrange("k p c -> p k c"), ut[b][:, :, :],
                  reads=[("d_ut", b)], writes=[("u2T", t)], sem=f"d_su{b}", eng="pool")


def phase_e(env):
    nc, P, NT, T = env["nc"], env["P"], env["NT"], env["T"]
    prm, epsc = env["prm"], env["epsc"]
    WIN = 384
    with ExitStack() as st:
        S = lambda name, shape, dt: st.enter_context(nc.sbuf_tensor(name, shape, dt))
        PS = lambda name, shape, dt: st.enter_context(nc.psum_tensor(name, shape, dt))
        wup = env["wup"]
        wdn = S("e_wdn", [128, 22, D], BF16)
        gf = S("e_gf", [128, D], F32)
        u2w = [S(f"e_u2w{i}", [128, 8, WIN + 2], BF16) for i in range(2)]
        actT = S("e_actT", [128, 22, WIN], BF16)
        yg = [S(f"e_yg{i}", [128, WIN], F32) for i in range(2)]
        yv = [S(f"e_yv{i}", [128, WIN], F32) for i in range(2)]
        sg = [S(f"e_sg{i}", [128, WIN], F32) for i in range(2)]
        hm = [S(f"e_hm{i}", [128, D], F32) for i in range(2)]
        ho = S("e_ho", [128, D], F32)
        yo = [S(f"e_yo{i}", [128, D], F32) for i in range(2)]
        junk = S("e_junk", [128, D], BF16)
        ss = S("e_ss", [128, 1], F32)
        rstd = S("e_rs", [128, 1], F32)
        pg = [PS(f"e_pg{i}", [128, 512], F32) for i in range(2)]
        pv = [PS(f"e_pv{i}", [128, 512], F32) for i in range(2)]
        pd = [PS(f"e_pd{i}", [128, 512], F32) for i in range(2)]
        P.dma(gf[:, :], env["gb"].ap()[:, 2 * D:3 * D], writes=["e_gf"], sem="l_gf")
        load_weight_bf16(P, nc, wdn, env["w_down"], 22, D, "e_wdn", "wE2", kgroup=6)
        UK = ["e_wup"] * 8

        wins = []
        a = 128
        while a < T:
            Ww = min(WIN, T - a)
            wins.append((a, Ww))
            a += Ww

        def loadw(wi):
            a, Ww = wins[wi]
            b = wi % 2
            P.dma(u2w[b][:, :, 0:Ww + 2], env["u2T"].ap()[:, :, a - 2:a + Ww].rearrange("k p c -> p k c"),
                  writes=[("e_u2w", b)], sem=f"e_lu{b}")

        loadw(0)
        it = 0
        tcount = 0
        for wi, (a, Ww) in enumerate(wins):
            b = wi % 2
            if wi + 1 < len(wins):
                loadw(wi + 1)
            for i in range(22):
                pb = it % 2
                it += 1
                for (ps_, col, key) in ((pg[pb], i * 128, ("e_pg", pb)), (pv[pb], D_FF + i * 128, ("e_pv", pb))):
                    for kc in range(8):
                        P.I("pe", "matmul", reads=[("e_u2w", b), UK[kc]], writes=[key],
                            out=ps_[:, 0:Ww + 2], lhsT=wup[:, kc, col:col + 128], rhs=u2w[b][:, kc, 0:Ww + 2],
                            start=(kc == 0), stop=(kc == 7))
                for (ps_, key, y, ykey, ct) in ((pg[pb], ("e_pg", pb), yg[pb], ("e_yg", pb), i),
                                                (pv[pb], ("e_pv", pb), yv[pb], ("e_yv", pb), 22 + i)):
                    w0 = prm[:, P_FW + ct * 3 + 0:P_FW + ct * 3 + 1]
                    w1 = prm[:, P_FW + ct * 3 + 1:P_FW + ct * 3 + 2]
                    w2 = prm[:, P_FW + ct * 3 + 2:P_FW + ct * 3 + 3]
                    bb = prm[:, P_FB + ct:P_FB + ct + 1]
                    P.I("act", "activation", reads=[key, "prm"], writes=[ykey],
                        out=y[:, 0:Ww], in_=ps_[:, 2:Ww + 2], func=AF.Identity, bias=bb, scale=w2)
                    P.I("dve", "scalar_tensor_tensor", reads=[key, "prm", ykey], writes=[ykey],
                        out=y[:, 0:Ww], in0=ps_[:, 1:Ww + 1], scalar=w1, in1=y[:, 0:Ww], op0=ALU.mult, op1=ALU.add)
                    P.I("dve", "scalar_tensor_tensor", reads=[key, "prm", ykey], writes=[ykey],
                        out=y[:, 0:Ww], in0=ps_[:, 0:Ww], scalar=w0, in1=y[:, 0:Ww], op0=ALU.mult, op1=ALU.add)
                P.I("act", "activation", reads=[("e_yg", pb)], writes=[("e_sg", pb)],
                    out=sg[pb][:, 0:Ww], in_=yg[pb][:, 0:Ww], func=AF.Silu)
                P.I("pool", "tensor_tensor", reads=[("e_sg", pb), ("e_yv", pb)], writes=[("e_actT", i)],
                    out=actT[:, i, 0:Ww], in0=sg[pb][:, 0:Ww], in1=yv[pb][:, 0:Ww], op=ALU.mult)
            AK = [("e_actT", i) for i in range(22)]
            for j in range(Ww // 128):
                tb = tcount % 2
                tcount += 1
                row0 = a + j * 128
                P.dma(hm[tb][:, :], env["hmid"].ap()[row0:row0 + 128, :], writes=[("e_hm", tb)], sem=f"e_lh{tb}")
                for half in range(2):
                    for i in range(22):
                        P.I("pe", "matmul", reads=[AK[i], "e_wdn"], writes=[("e_pd", half)],
                            out=pd[half][:, :], lhsT=actT[:, i, j * 128:(j + 1) * 128],
                            rhs=wdn[:, i, half * 512:(half + 1) * 512], start=(i == 0), stop=(i == 21))
                    P.I("dve", "tensor_tensor", reads=[("e_pd", half), ("e_hm", tb)], writes=[("e_ho", half)],
                        out=ho[:, half * 512:(half + 1) * 512], in0=pd[half][:, :],
                        in1=hm[tb][:, half * 512:(half + 1) * 512], op=ALU.add)
                HO = [("e_ho", 0), ("e_ho", 1)]
                P.I("act", "activation", reads=HO, writes=["e_junk", "e_ss"],
                    out=junk[:, :], in_=ho[:, :], func=AF.Square, accum_out=ss[:, :])
                P.I("act", "activation", reads=["e_ss", "epsc"], writes=["e_rs"],
                    out=rstd[:, :], in_=ss[:, :], func=AF.Sqrt, bias=epsc[:, 0:1], scale=1.0 / D)
                P.I("dve", "reciprocal", reads=["e_rs"], writes=["e_rs"], out=rstd[:, :], in_=rstd[:, :])
                P.I("dve", "scalar_tensor_tensor", reads=HO + ["e_rs", "e_gf"], writes=[("e_yo", tb)],
                    out=yo[tb][:, :], in0=ho[:, :], scalar=rstd[:, 0:1], in1=gf[:, :], op0=ALU.mult, op1=ALU.mult)
                P.dma(env["out"].ap()[row0 - 128:row0, :], yo[tb][:, :], reads=[("e_yo", tb)],
                      writes=[("out", row0)], sem=f"e_so{tb}")


def _t5_bucket_np(rel):
    nb = 16
    max_exact = 8
    rel = np.asarray(rel, np.int32)
    ret = np.where(rel > 0, nb, 0)
    n = np.abs(rel)
    nf = np.maximum(n, 1).astype(np.float32)
    large = max_exact + (np.log(nf / np.float32(max_exact)) / np.float32(math.log(128 / max_exact))
                         * np.float32(nb - max_exact)).astype(np.int32)
    large = np.minimum(large, nb - 1)
    return ret + np.where(n < max_exact, n, large)


def make_consts():
    c = np.zeros((128, NCST), np.float32)
    i = np.arange(128)
    c[:, C_ID:C_ID + 128] = np.eye(128, dtype=np.float32)
    c[:, C_J:C_J + 128] = np.eye(128, dtype=np.float32)[::-1]
    c[:, C_U:C_U + 128] = (i[:, None] <= i[None, :]).astype(np.float32)
    c[:, C_MSL:C_MSL + 128] = np.where(i[None, :] < i[:, None], 0.0, NEG)
    c[:, C_MUI:C_MUI + 128] = np.where(i[None, :] >= i[:, None], 0.0, NEG)
    c[:, C_ONE:C_ONE + 128] = 1.0
    c[:, C_PMSL:C_PMSL + 128] = np.where(i[None, :] < i[:, None], 0.0, -NEG)
    r = np.arange(512)
    b = _t5_bucket_np(256 - r)
    c[b, C_E1 + r] = 1.0
    return c


def make_prm(inp):
    p = np.zeros((128, NPRM), np.float32)
    p[:, P_GSUB] = inp["da_subln_g"][0]
    p[:, P_GDN] = inp["dn_norm_g"][0]
    p[:, P_ALOG:P_ALOG + 4] = inp["dn_A_log"][0][None, :]
    p[:, P_DTB:P_DTB + 4] = inp["dn_dt_bias"][0][None, :]
    p[:, P_C15:P_C15 + 4] = inp["rel_bias"][15][None, :]
    cw = inp["dn_conv_w"][0]
    p[:, P_CW:P_CW + 48] = cw.reshape(4, 12, 128).transpose(2, 1, 0).reshape(128, 48)
    fw = inp["ffn_conv_w"][0]
    p[:, P_FW:P_FW + 132] = fw.reshape(3, 44, 128).transpose(2, 1, 0).reshape(128, 132)
    p[:, P_FB:P_FB + 44] = inp["ffn_conv_b"][0].reshape(44, 128).T
    p[:, P_LAM:P_LAM + 256] = inp["da_lambda"][0].reshape(1, 256)
    return p


def make_gb(inp):
    g = np.zeros((128, 3 * D), np.float32)
    g[:, 0:D] = inp["norm1_g"][0][None, :]
    g[:, D:2 * D] = inp["norm2_g"][0][None, :]
    g[:, 2 * D:] = inp["final_norm_g"][None, :]
    return g


def make_hin(xb, meta, NT):
    h = np.zeros((NT * 128, D), np.float32)
    h[112:128] = meta
    h[128:] = xb[:(NT - 1) * 128]
    return h


_NC_CACHE = {}


def kernel(**inp):
    inp = {k: np.asarray(v) for k, v in inp.items()}
    x = inp["x"]
    B = x.shape[0]
    NT = NT_FULL
    if NT not in _NC_CACHE:
        _NC_CACHE[NT] = build(NT)
    nc = _NC_CACHE[NT]
    cst = make_consts()
    prm = make_prm(inp)
    gb = make_gb(inp)
    shared = dict(w_in=np.ascontiguousarray(inp["w_in"][0]), w_out=np.ascontiguousarray(inp["w_out"][0]),
                  w_up=np.ascontiguousarray(inp["w_up"][0]), w_down=np.ascontiguousarray(inp["w_down"][0]),
                  gb=gb, prm=prm, cst=cst, relb=np.ascontiguousarray(inp["rel_bias"]))
    in_maps = []
    for b in range(B):
        m = dict(shared)
        m["hin"] = make_hin(x[b], inp["meta_tokens"], NT)
        in_maps.append(m)
    res = run_bass_kernel_spmd(nc, in_maps, core_ids=list(range(B)))
    return np.stack([np.asarray(r["out"]) for r in res.results], axis=0).astype(np.float32)
```

```python
import math
from contextlib import ExitStack
import numpy as np
import concourse.bass as bass
import concourse.mybir as mybir
from concourse.bass_utils import run_bass_kernel_spmd

F32 = mybir.dt.float32
BF16 = mybir.dt.bfloat16
AF = mybir.ActivationFunctionType
ALU = mybir.AluOpType

D = 1024
NT_FULL = 33
EPS = 1e-6
IN_COLS = 3592
D_FF = 2816
NEG = -1.0e30

P_GSUB = 0
P_GDN = 1
P_ALOG = 2
P_DTB = 6
P_C15 = 10
P_CW = 14
P_FW = 62
P_FB = 194
P_LAM = 238
NPRM = 494
C_ID = 0
C_J = 128
C_U = 256
C_MSL = 384
C_MUI = 512
C_ONE = 640
C_E1 = 768
C_PMSL = 1280
NCST = 1408
FV_OFF = 255


PSUM_NAMES = {"a_pT", "a_pm", "a_pba", "b_p", "b_pG", "c_pS", "c_pO", "c_pZ", "c_pN",
              "d_pm", "d_pT", "e_pg", "e_pv", "e_pd"}


class Op:
    __slots__ = ("eng", "fn", "deps", "is_dma", "semkey", "signal", "sig", "phase")


class Prog:
    ENGS = ("pe", "act", "dve", "pool", "sp")

    def __init__(self, nc, stack):
        self.nc = nc
        self.stack = stack
        self.esem = {e: stack.enter_context(nc.semaphore("s_" + e)) for e in self.ENGS}
        self.ecount = {e: 0 for e in self.ENGS}
        self.dsem = {}
        self.dcount = {}
        self.waited = {e: {} for e in self.ENGS}
        self.lastw = {}
        self.readers = {}
        self.ops = []
        self.phase = 0
        self.nops = 0

    def op(self, eng, fn, reads=(), writes=(), dma=None):
        o = Op()
        o.eng = eng
        o.fn = fn
        o.is_dma = dma is not None
        o.semkey = dma
        o.signal = o.is_dma
        o.sig = None
        o.phase = self.phase
        deps = []
        for k in reads:
            w = self.lastw.get(k)
            if w is not None and w.phase == self.phase:
                deps.append(w)
            if (k if isinstance(k, str) else k[0]) in PSUM_NAMES:
                for r in self.readers.get(k, ()):
                    if r.phase == self.phase and r.eng != eng:
                        deps.append(r)
        for k in writes:
            w = self.lastw.get(k)
            if w is not None and w.phase == self.phase:
                deps.append(w)
            for r in self.readers.get(k, ()):
                if r.phase == self.phase:
                    deps.append(r)
        for k in reads:
            self.readers.setdefault(k, []).append(o)
        for k in writes:
            self.lastw[k] = o
            self.readers[k] = []
        o.deps = [d for d in dict.fromkeys(deps) if d is not o]
        self.ops.append(o)
        return o

    def I(self, eng, name, reads=(), writes=(), **kw):
        return self.op(eng, lambda e: getattr(e, name)(**kw), reads, writes)

    def dma(self, out, in_, reads=(), writes=(), sem="d0", eng="sp"):
        if sem not in self.dsem:
            self.dsem[sem] = self.stack.enter_context(self.nc.semaphore("d_" + sem))
            self.dcount[sem] = 0
        return self.op(eng, lambda e: e.dma_start(out=out, in_=in_), reads, writes, dma=sem)

    def flush(self, final=False):
        ops = self.ops
        self.ops = []
        last = {}
        for o in ops:
            last[o.eng] = o
            for d in o.deps:
                if not d.is_dma and not (d.eng == "pe" and o.eng == "pe"):
                    d.signal = True
        for o in last.values():
            o.signal = True
        start_e = dict(self.ecount)
        start_d = dict(self.dcount)
        for o in ops:
            if o.is_dma:
                self.dcount[o.semkey] += 16
                o.sig = self.dcount[o.semkey]
            elif o.signal:
                self.ecount[o.eng] += 1
                o.sig = self.ecount[o.eng]
        handles = {}
        with self.nc.Block() as block:
            reg = {"pe": block.tensor, "act": block.scalar, "dve": block.vector,
                   "pool": block.gpsimd, "sp": block.sync}
            for e in self.ENGS:
                mine = [o for o in ops if o.eng == e]

                def body(eh, e=e, mine=mine):
                    wt = self.waited[e]
                    for e2 in self.ENGS:
                        if e2 != e and start_e[e2] > wt.get(e2, 0):
                            eh.wait_ge(self.esem[e2], start_e[e2])
                            wt[e2] = start_e[e2]
                    for k, v in start_d.items():
                        if v > wt.get("d_" + k, 0):
                            eh.wait_ge(self.dsem[k], v)
                            wt["d_" + k] = v
                    for o in mine:
                        for d in o.deps:
                            if d.is_dma:
                                key = "d_" + d.semkey
                                if d.sig > wt.get(key, 0):
                                    eh.wait_ge(self.dsem[d.semkey], d.sig)
                                    wt[key] = d.sig
                            else:
                                if d.eng == "pe" and e == "pe":
                                    continue
                                if d.sig > wt.get(d.eng, 0):
                                    eh.wait_ge(self.esem[d.eng], d.sig)
                                    wt[d.eng] = d.sig
                        ins = o.fn(eh)
                        if o.is_dma:
                            ins.then_inc(self.dsem[o.semkey], 16)
                        elif o.signal:
                            ins.then_inc(self.esem[e], 1)
                    if final:
                        for k, v in self.dcount.items():
                            if v > wt.get("d_" + k, 0):
                                eh.wait_ge(self.dsem[k], v)
                                wt["d_" + k] = v
                        for e2 in self.ENGS:
                            if e2 != e and self.ecount[e2] > wt.get(e2, 0):
                                eh.wait_ge(self.esem[e2], self.ecount[e2])
                                wt[e2] = self.ecount[e2]

                reg[e](body)
        self.nops += len(ops)
        self.phase += 1


def dram_ap(t, off, pattern):
    return bass.AP(t, off, [list(p) for p in pattern])


def build(NT=NT_FULL, debug=False, stop_after=None):
    T = NT * 128
    nc = bass.Bass("TRN2", target_bir_lowering=False)
    okind = "ExternalOutput" if debug else None

    def dt_(name, shape, dt, kind=None):
        if kind is None:
            return nc.dram_tensor(name, shape, dt)
        return nc.dram_tensor(name, shape, dt, kind=kind)

    hin = dt_("hin", [T, D], F32, "ExternalInput")
    w_in = dt_("w_in", [D, IN_COLS], F32, "ExternalInput")
    w_out = dt_("w_out", [D, D], F32, "ExternalInput")
    w_up = dt_("w_up", [D, 2 * D_FF], F32, "ExternalInput")
    w_down = dt_("w_down", [D_FF, D], F32, "ExternalInput")
    gb = dt_("gb", [128, 3 * D], F32, "ExternalInput")
    prm = dt_("prm", [128, NPRM], F32, "ExternalInput")
    cst = dt_("cst", [128, NCST], F32, "ExternalInput")
    relb = dt_("relb", [32, 4], F32, "ExternalInput")
    out = dt_("out", [(NT - 1) * 128, D], F32, "ExternalOutput")

    qT_da = dt_("qT_da", [4, 128, T], BF16, okind)
    kT_da = dt_("kT_da", [4, 128, T], BF16, okind)
    v_da = dt_("v_da", [T, 512], BF16, okind)
    qkvT = dt_("qkvT", [12, 128, T], F32, okind)
    szT = dt_("szT", [4, 128, T], F32, okind)
    ba = dt_("ba", [T, 8], F32, okind)
    oT = dt_("oT", [8, 128, T], BF16, okind)
    hmid = dt_("hmid", [T, D], F32, okind)
    u2T = dt_("u2T", [8, 128, T], BF16, okind)
    fvec = dt_("fvec", [4, 512], F32, okind)

    with ExitStack() as top:
        P = Prog(nc, top)
        prm_t = top.enter_context(nc.sbuf_tensor("prm_t", [128, NPRM], F32))
        cst_t = top.enter_context(nc.sbuf_tensor("cst_t", [128, NCST], F32))
        idb = top.enter_context(nc.sbuf_tensor("idb", [128, 128], BF16))
        oneb = top.enter_context(nc.sbuf_tensor("oneb", [128, 128], BF16))
        P.dma(prm_t[:, :], prm.ap()[:, :], writes=["prm"], sem="l_prm")
        P.dma(cst_t[:, :], cst.ap()[:, :], writes=["cst"], sem="l_cst")
        epsc = top.enter_context(nc.sbuf_tensor("epsc", [128, 2], F32))
        P.op("dve", lambda e: e.memset(epsc[:, 0:1], EPS), writes=["epsc"])
        P.op("dve", lambda e: e.memset(epsc[:, 1:2], 128.0 * EPS), writes=["epsc"])
        P.op("dve", lambda e: e.tensor_copy(out=idb[:, :], in_=cst_t[:, C_ID:C_ID + 128]),
             reads=["cst"], writes=["idb"])
        P.op("dve", lambda e: e.tensor_copy(out=oneb[:, :], in_=cst_t[:, C_ONE:C_ONE + 128]),
             reads=["cst"], writes=["oneb"])

        env = dict(nc=nc, P=P, NT=NT, T=T, hin=hin, w_in=w_in, w_out=w_out, w_up=w_up,
                   w_down=w_down, gb=gb, prm=prm_t, cst=cst_t, relb=relb, out=out,
                   qT_da=qT_da, kT_da=kT_da, v_da=v_da, qkvT=qkvT, szT=szT, ba=ba, oT=oT,
                   hmid=hmid, u2T=u2T, fvec=fvec, idb=idb, oneb=oneb, epsc=epsc)
        phases = [("A", phase_a), ("B", phase_b), ("C", phase_c), ("D", phase_d), ("E", phase_e)]
        for i, (name, fn) in enumerate(phases):
            lastp = (i == len(phases) - 1) or (stop_after == name)
            if name == "C":
                env["wup"] = top.enter_context(nc.sbuf_tensor("e_wup", [128, 8, 2 * D_FF], BF16))
            fn(env)
            P.flush(final=lastp)
            if lastp:
                break
    return nc


def load_weight_bf16(P, nc, wt, w_dram, nk, ncols, key, sem, kgroup=4, step=2048):
    for k0 in range(0, nk, kgroup):
        k1 = min(nk, k0 + kgroup)
        for c0 in range(0, ncols, step):
            c1 = min(ncols, c0 + step)
            P.dma(wt[:, k0:k1, c0:c1],
                  w_dram.ap()[k0 * 128:k1 * 128, c0:c1].rearrange("(k p) c -> p k c", p=128),
                  writes=[key], sem=sem, eng="pool")


def rmsnorm_rows(P, x_t, xkey, ss_t, sskey, rstd_t, rkey, junk_t, jkey, epsc, n=D):
    P.op("act", lambda e: e.activation(out=junk_t, in_=x_t, func=AF.Square, accum_out=ss_t),
         reads=[xkey], writes=[jkey, sskey])
    P.op("act", lambda e: e.activation(out=rstd_t, in_=ss_t, func=AF.Sqrt, bias=epsc[:, 0:1], scale=1.0 / n),
         reads=[sskey, "epsc"], writes=[rkey])
    P.op("dve", lambda e: e.reciprocal(out=rstd_t, in_=rstd_t),
         reads=[rkey], writes=[rkey])


def phase_a(env):
    nc, P, NT, T = env["nc"], env["P"], env["NT"], env["T"]
    hin, idb = env["hin"], env["idb"]
    with ExitStack() as st:
        S = lambda name, shape, dt: st.enter_context(nc.sbuf_tensor(name, shape, dt))
        PS = lambda name, shape, dt: st.enter_context(nc.psum_tensor(name, shape, dt))
        win = S("a_win", [128, 8, IN_COLS], BF16)
        g1 = S("a_g1", [128, D], F32)
        ht = [S(f"a_h{i}", [128, D], F32) for i in range(2)]
        junk = S("a_junk", [128, D], BF16)
        ss = [S(f"a_ss{i}", [128, 1], F32) for i in range(2)]
        rstd = [S(f"a_rs{i}", [128, 1], F32) for i in range(2)]
        ub = [S(f"a_u{i}", [128, D], BF16) for i in range(2)]
        uT = [S(f"a_uT{i}", [128, 8, 512], BF16) for i in range(2)]
        evb = [S(f"a_evb{i}", [128, 512], BF16) for i in range(3)]
        evf = [S(f"a_evf{i}", [128, 512], F32) for i in range(3)]
        bat = [S(f"a_ba{i}", [128, 8], F32) for i in range(2)]
        pT = [PS(f"a_pT{i}", [128, 8, 128], BF16) for i in range(2)]
        pm = [PS(f"a_pm{i}", [128, 512], F32) for i in range(4)]
        pba = PS("a_pba", [128, 8], F32)

        P.dma(g1[:, :], env["gb"].ap()[:, 0:D], writes=["a_g1"], sem="l_g1")
        WBLK = [(0, 1536), (1536, IN_COLS)]
        for bi_, (cA, cB) in enumerate(WBLK):
            for k0 in (0, 4):
                P.dma(win[:, k0:k0 + 4, cA:cB],
                      env["w_in"].ap()[k0 * 128:(k0 + 4) * 128, cA:cB].rearrange("(k p) c -> p k c", p=128),
                      writes=[("a_win", bi_)], sem=f"wA{bi_}", eng="pool")

        def wkeys(cA, cB):
            return [("a_win", i) for i, (x0, x1) in enumerate(WBLK) if x0 < cB and cA < x1]

        nsup = (NT + 3) // 4
        pmi = 0
        evi = 0

        def load_tile(t):
            P.dma(ht[t % 2][:, :], hin.ap()[t * 128:(t + 1) * 128, :],
                  writes=[("a_h", t % 2)], sem=f"a_h{t % 2}")

        def prep_tile(t):
            s_, j = t // 4, t % 4
            us_ = uT[s_ % 2]
            b = t % 2
            if t + 1 < NT:
                load_tile(t + 1)
            rmsnorm_rows(P, ht[b][:, :], ("a_h", b), ss[b][:, :], ("a_ss", b),
                         rstd[b][:, :], ("a_rs", b), junk[:, :], "a_junk", env["epsc"])
            P.I("dve", "scalar_tensor_tensor", reads=[("a_h", b), ("a_rs", b), "a_g1"], writes=[("a_u", b)],
                out=ub[b][:, :], in0=ht[b][:, :], scalar=rstd[b][:, 0:1], in1=g1[:, :], op0=ALU.mult, op1=ALU.mult)

        def prep_tile_b(t):
            s_, j = t // 4, t % 4
            us_ = uT[s_ % 2]
            b = t % 2
            for kc in range(8):
                P.I("pe", "transpose", reads=[("a_u", b), "idb"], writes=[("a_pT", b)],
                    out=pT[b][:, kc, :], in_=ub[b][:, kc * 128:(kc + 1) * 128], identity=idb[:, :])
            P.I("act", "copy", reads=[("a_pT", b)], writes=[("a_uT", s_ % 2, j)],
                out=us_[:, :, j * 128:(j + 1) * 128], in_=pT[b][:, :, :])

        load_tile(0)
        for t in range(min(4, NT)):
            prep_tile(t)
            prep_tile_b(t)
        for s in range(nsup):
            t0 = s * 4
            ntile = min(4, NT - t0)
            W = ntile * 128
            us = uT[s % 2]
            nxt_tiles = list(range(t0 + 4, min(t0 + 8, NT)))
            gcount = 0
            UK = [("a_uT", s % 2, j) for j in range(ntile)]
            c0 = t0 * 128
            fm = []
            for h in range(4):
                fm.append((h * 128, "q", h))
            for h in range(4):
                fm.append((512 + h * 128, "k", h))
            for ct in range(12):
                fm.append((1536 + ct * 128, "x", ct))
            for h in range(4):
                fm.append((3072 + h * 128, "z", h))
            for (col, kind, idx) in fm:
                gcount += 1
                if gcount % 6 == 1 and nxt_tiles:
                    prep_tile(nxt_tiles[0])
                if gcount % 6 == 4 and nxt_tiles:
                    prep_tile_b(nxt_tiles.pop(0))
                pb = pmi % 4
                pmi += 1
                for kc in range(8):
                    P.op("pe", lambda e, pb=pb, kc=kc, col=col, us=us, W=W: e.matmul(
                        out=pm[pb][:, 0:W], lhsT=win[:, kc, col:col + 128], rhs=us[:, kc, 0:W],
                        start=(kc == 0), stop=(kc == 7)),
                        reads=UK + wkeys(col, col + 128), writes=[("a_pm", pb)])
                eb = evi % 3
                evi += 1
                if kind in ("q", "k"):
                    dst = env["qT_da"] if kind == "q" else env["kT_da"]
                    P.op("act", lambda e, pb=pb, eb=eb, W=W: e.copy(out=evb[eb][:, 0:W], in_=pm[pb][:, 0:W]),
                         reads=[("a_pm", pb)], writes=[("a_evb", eb)])
                    P.dma(dst.ap()[idx, :, c0:c0 + W], evb[eb][:, 0:W], reads=[("a_evb", eb)],
                          writes=[(kind + "T_da", idx, s)], sem=f"a_sb{eb}")
                elif kind == "x":
                    P.op("dve", lambda e, pb=pb, eb=eb, W=W: e.tensor_copy(out=evf[eb][:, 0:W], in_=pm[pb][:, 0:W]),
                         reads=[("a_pm", pb)], writes=[("a_evf", eb)])
                    P.dma(env["qkvT"].ap()[idx, :, c0:c0 + W], evf[eb][:, 0:W], reads=[("a_evf", eb)],
                          writes=[("qkvT", idx, s)], sem=f"a_sf{eb}")
                else:
                    P.op("act", lambda e, pb=pb, eb=eb, W=W: e.activation(
                        out=evf[eb][:, 0:W], in_=pm[pb][:, 0:W], func=AF.Silu),
                        reads=[("a_pm", pb)], writes=[("a_evf", eb)])
                    P.dma(env["szT"].ap()[idx, :, c0:c0 + W], evf[eb][:, 0:W], reads=[("a_evf", eb)],
                          writes=[("szT", idx, s)], sem=f"a_sf{eb}")
            while nxt_tiles:
                t_ = nxt_tiles.pop(0)
                prep_tile(t_)
                prep_tile_b(t_)
            for j in range(ntile):
                t = t0 + j
                pb = pmi % 4
                pmi += 1
                for kc in range(8):
                    P.op("pe", lambda e, pb=pb, kc=kc, us=us, j=j: e.matmul(
                        out=pm[pb][:, :], lhsT=us[:, kc, j * 128:(j + 1) * 128], rhs=win[:, kc, 1024:1536],
                        start=(kc == 0), stop=(kc == 7)),
                        reads=[("a_uT", s % 2, j)] + wkeys(1024, 1536), writes=[("a_pm", pb)])
                eb = evi % 3
                evi += 1
                P.op("act", lambda e, pb=pb, eb=eb: e.copy(out=evb[eb][:, :], in_=pm[pb][:, :]),
                     reads=[("a_pm", pb)], writes=[("a_evb", eb)])
                P.dma(env["v_da"].ap()[t * 128:(t + 1) * 128, :], evb[eb][:, :], reads=[("a_evb", eb)],
                      writes=[("v_da", t)], sem=f"a_sb{eb}")
                for kc in range(8):
                    P.op("pe", lambda e, kc=kc, us=us, j=j: e.matmul(
                        out=pba[:, :], lhsT=us[:, kc, j * 128:(j + 1) * 128], rhs=win[:, kc, 3584:3592],
                        start=(kc == 0), stop=(kc == 7)),
                        reads=[("a_uT", s % 2, j)] + wkeys(3584, 3592), writes=["a_pba"])
                P.op("dve", lambda e, t=t: e.tensor_copy(out=bat[t % 2][:, :], in_=pba[:, :]),
                     reads=["a_pba"], writes=[("a_bat", t % 2)])
                P.dma(env["ba"].ap()[t * 128:(t + 1) * 128, :], bat[t % 2][:, :], reads=[("a_bat", t % 2)],
                      writes=[("ba", t)], sem=f"a_sba{t % 2}")


def phase_b(env):
    nc, P, NT, T = env["nc"], env["P"], env["NT"], env["T"]
    prm, cst, epsc, oneb, idb = env["prm"], env["cst"], env["epsc"], env["oneb"], env["idb"]
    H4 = range(4)
    DN = F32
    DO = BF16
    DK = BF16
    DX = F32
    with ExitStack() as st:
        S = lambda name, shape, dt: st.enter_context(nc.sbuf_tensor(name, shape, dt))
        PS = lambda name, shape, dt: st.enter_context(nc.psum_tensor(name, shape, dt))
        ba_all = S("b_ba", [128, NT, 8], F32)
        beta = S("b_beta", [128, NT, 4], F32)
        nbeta = S("b_nbeta", [128, NT, 4], F32)
        gg = S("b_g", [128, NT, 4], F32)
        etmp = S("b_etmp", [128, NT], F32)
        negA = S("b_negA", [128, 4], F32)
        pmsl4 = S("b_pmsl4", [128, 4, 128], F32)
        mui4 = S("b_mui4", [128, 4, 128], F32)
        id4 = S("b_id4", [128, 4, 128], F32)
        xin = [S(f"b_xin{i}", [128, 515], F32) for i in range(2)]
        ycv = [S(f"b_y{i}", [128, 512], F32) for i in range(2)]
        ys = [S(f"b_ys{i}", [128, 512], F32) for i in range(8)]
        qkbs = [S(f"b_qkb{i}", [128, 12, 512], DK) for i in range(2)]
        sq = [S(f"b_sq{i}", [128, 512], BF16) for i in range(2)]
        rn = [S(f"b_rn{i}", [128, 512], F32) for i in range(2)]
        szs = [S(f"b_sz{i}", [128, 4, 512], F32) for i in range(3)]
        def bufset(n):
            d = {}
            d["n"] = n
            d["Gt"] = S(f"b_Gt{n}", [128, 8], F32)
            d["gU"] = S(f"b_gU{n}", [128, 4, 128], F32)
            d["tA"] = S(f"b_tA{n}", [128, 4, 128], F32)
            d["tAT"] = S(f"b_tAT{n}", [128, 4, 128], F32)
            d["Dm"] = S(f"b_Dm{n}", [128, 4, 128], F32)
            d["DT"] = S(f"b_DT{n}", [128, 4, 128], F32)
            d["eGr"] = S(f"b_eGr{n}", [128, 4, 128], F32)
            d["X"] = [S(f"b_X{n}{i}", [128, 4, 128], DX) for i in range(2)]
            d["XT"] = [S(f"b_XT{n}{i}", [128, 4, 128], DX) for i in range(2)]
            d["IX"] = S(f"b_IX{n}", [128, 4, 128], DN)
            d["Pt"] = [S(f"b_Pt{n}{i}", [128, 4, 128], DN) for i in range(2)]
            d["kbgn"] = S(f"b_kbgn{n}", [128, 4, 128], DN)
            return d

        def hoset(n):
            d = {}
            d["n"] = n
            d["sc"] = S(f"b_sc{n}", [128, 16], F32)
            d["Ptf"] = S(f"b_Ptf{n}", [128, 4, 128], DN)
            d["AqkT"] = S(f"b_AqkT{n}", [128, 4, 128], DO)
            d["kg"] = S(f"b_kg{n}", [128, 4, 128], DO)
            d["vb"] = S(f"b_vb{n}", [128, 4, 128], DN)
            d["WTn"] = S(f"b_WTn{n}", [128, 4, 128], DN)
            d["qgT"] = S(f"b_qgT{n}", [128, 4, 128], DO)
            return d

        HO = [hoset(i) for i in range(4)]
        BS = [bufset(0), bufset(1)]
        vnew = S("b_vnew", [128, 4, 128], DO)
        St = S("b_S", [128, 4, 128], F32)
        Sb = S("b_Sb", [128, 4, 128], DO)
        osq = S("b_osq", [128, 4, 128], BF16)
        rno = S("b_rno", [128, 4, 128], F32)
        t1 = S("b_t1", [128, 4, 128], F32)
        ob = [S(f"b_ob{i}", [128, 4, 128], BF16) for i in range(2)]
        pb = [PS(f"b_p{i}", [128, 4, 128], F32) for i in range(7)]
        pcount = [0]

        live = [False] * 7

        def nxt():
            for _ in range(7):
                i = pcount[0] % 7
                pcount[0] += 1
                if not live[i]:
                    live[i] = True
                    return pb[i], ("b_p", i)
            raise AssertionError("all GDN PSUM banks are live")

        def done(key):
            live[key[1]] = False

        def bfv(p, dt=BF16):
            return p[:, :, :].bitcast(BF16) if dt == BF16 else p[:, :, :]

        ONES = cst[:, C_ONE:C_ONE + 128]
        UM = cst[:, C_U:C_U + 128]
        IDF = cst[:, C_ID:C_ID + 128]

        for h in H4:
            P.I("pool", "tensor_copy", reads=["cst"], writes=["b_pmsl4"], out=pmsl4[:, h, :], in_=cst[:, C_PMSL:C_PMSL + 128])
            P.I("pool", "tensor_copy", reads=["cst"], writes=["b_mui4"], out=mui4[:, h, :], in_=cst[:, C_MUI:C_MUI + 128])
            P.I("pool", "tensor_copy", reads=["cst"], writes=["b_id4"], out=id4[:, h, :], in_=IDF)
        P.I("pool", "memset", writes=["b_S"], ap=St[:, :, :], constant=0.0)
        P.I("pool", "memset", writes=["b_Sb"], ap=Sb[:, :, :], constant=0.0)
        P.dma(ba_all[:, :, :], env["ba"].ap()[:, :].rearrange("(t p) c -> p t c", p=128), writes=["b_ba"], sem="b_l0")
        P.I("act", "activation", reads=["b_ba"], writes=["b_beta"], out=beta[:, :, :], in_=ba_all[:, :, 0:4], func=AF.Sigmoid)
        P.I("dve", "tensor_scalar", reads=["b_beta"], writes=["b_nbeta"], out=nbeta[:, :, :], in0=beta[:, :, :],
            scalar1=-1.0, scalar2=None, op0=ALU.mult)
        P.I("act", "activation", reads=["prm"], writes=["b_negA"], out=negA[:, :], in_=prm[:, P_ALOG:P_ALOG + 4], func=AF.Exp)
        P.I("dve", "tensor_scalar", reads=["b_negA"], writes=["b_negA"], out=negA[:, :], in0=negA[:, :],
            scalar1=-1.0, scalar2=None, op0=ALU.mult)
        for h in H4:
            P.I("act", "activation", reads=["b_ba", "prm"], writes=["b_etmp"], out=etmp[:, :], in_=ba_all[:, :, 4 + h],
                func=AF.Exp, bias=prm[:, P_DTB + h:P_DTB + h + 1], scale=1.0)
            P.I("act", "activation", reads=["b_etmp", "cst"], writes=["b_etmp"], out=etmp[:, :], in_=etmp[:, :],
                func=AF.Ln, bias=cst[:, C_ONE:C_ONE + 1], scale=1.0)
            P.I("dve", "tensor_scalar", reads=["b_etmp", "b_negA"], writes=["b_g"], out=gg[:, :, h], in0=etmp[:, :],
                scalar1=negA[:, h:h + 1], scalar2=None, op0=ALU.mult)

        nsup = (NT + 3) // 4
        xctr = [0]

        def conv_gen(s):
            t0 = s * 4
            ntile = min(4, NT - t0)
            W = ntile * 128
            c0 = t0 * 128
            sp = s % 2
            qkb = qkbs[sp]
            P.dma(szs[s % 3][:, :, 0:W], env["szT"].ap()[:, :, c0:c0 + W].rearrange("k p c -> p k c"), writes=[("b_sz", s % 3)], sem=f"b_l1{s % 3}")
            for ct in range(12):
                xb = xctr[0] % 2
                xctr[0] += 1
                xk = ("b_xin", xb)
                if s == 0:
                    P.I("pool", "memset", writes=[xk], ap=xin[xb][:, 0:3], constant=0.0)
                    P.dma(xin[xb][:, 3:3 + W], env["qkvT"].ap()[ct, :, 0:W], writes=[xk], sem=f"b_lx{xb}")
                else:
                    P.dma(xin[xb][:, 0:3 + W], env["qkvT"].ap()[ct, :, c0 - 3:c0 + W], writes=[xk], sem=f"b_lx{xb}")
                y = ycv[xb]
                yk = ("b_y", xb)
                wc = lambda tap: prm[:, P_CW + ct * 4 + tap:P_CW + ct * 4 + tap + 1]
                P.I("act", "activation", reads=[xk, "prm"], writes=[yk], out=y[:, 0:W], in_=xin[xb][:, 3:3 + W],
                    func=AF.Identity, scale=wc(3))
                yield
                for tap in (2, 1, 0):
                    P.I("dve", "scalar_tensor_tensor", reads=[xk, "prm", yk], writes=[yk], out=y[:, 0:W],
                        in0=xin[xb][:, tap:tap + W], scalar=wc(tap), in1=y[:, 0:W], op0=ALU.mult, op1=ALU.add)
                yield
                if ct >= 8:
                    P.I("act", "activation", reads=[yk], writes=[("b_qkb", sp, ct)], out=qkb[:, ct, 0:W], in_=y[:, 0:W], func=AF.Silu)
                else:
                    P.I("act", "activation", reads=[yk], writes=[("b_ys", ct)], out=ys[ct][:, 0:W], in_=y[:, 0:W], func=AF.Silu)
                yield
            for ct in range(8):
                xb = ct % 2
                ysb = ys[ct]
                ysk = ("b_ys", ct)
                qk_ = ("b_qkb", sp, ct)
                P.I("pool", "tensor_tensor", reads=[ysk], writes=[("b_sq", xb)], out=sq[xb][:, 0:W],
                    in0=ysb[:, 0:W], in1=ysb[:, 0:W], op=ALU.mult)
                yield
                pp, pk = nxt()
                P.I("pe", "matmul", reads=["oneb", ("b_sq", xb)], writes=[pk], out=pp[:, 0:ntile, :],
                    lhsT=oneb[:, :], rhs=sq[xb][:, 0:W], start=True, stop=True)
                yield
                P.I("act", "activation", reads=[pk, "epsc"], writes=[("b_rn", xb)], out=rn[xb][:, 0:W],
                    in_=pp[:, 0:ntile, :], func=AF.Ln, bias=epsc[:, 0:1], scale=1.0)
                done(pk)
                P.I("act", "activation", reads=[("b_rn", xb)], writes=[("b_rn", xb)], out=rn[xb][:, 0:W],
                    in_=rn[xb][:, 0:W], func=AF.Exp, scale=-0.5)
                yield
                if ct < 4:
                    P.I("dve", "scalar_tensor_tensor", reads=[ysk, ("b_rn", xb)], writes=[qk_], out=qkb[:, ct, 0:W],
                        in0=ysb[:, 0:W], scalar=128 ** -0.5, in1=rn[xb][:, 0:W], op0=ALU.mult, op1=ALU.mult)
                else:
                    P.I("dve", "tensor_tensor", reads=[ysk, ("b_rn", xb)], writes=[qk_], out=qkb[:, ct, 0:W],
                        in0=ysb[:, 0:W], in1=rn[xb][:, 0:W], op=ALU.mult)
                yield

        def drive(gens):
            gens = list(gens)
            while gens:
                for g in list(gens):
                    try:
                        next(g)
                    except StopIteration:
                        gens.remove(g)

        drive([conv_gen(0)])
        state_b = dict(pair=0, prev=None)
        for s in range(nsup):
            t0 = s * 4
            ntile = min(4, NT - t0)
            W = ntile * 128
            c0 = t0 * 128
            sp = s % 2
            qkb = qkbs[sp]
            QK = [("b_qkb", sp, ct) for ct in range(12)]
            nextconv = [conv_gen(s + 1)] if s + 1 < nsup else []

            def chunk_prep(j, bs, ho, qkb=qkb, QK=QK, t0=t0):
                t = t0 + j
                n = bs["n"]
                hn = ho["n"]
                cs = slice(j * 128, (j + 1) * 128)
                Gt, gU, tA, tAT, Dm, DT, eGr = (bs[k] for k in ("Gt", "gU", "tA", "tAT", "Dm", "DT", "eGr"))
                X, XT, IX, Pt, kbgn = (bs[k] for k in ("X", "XT", "IX", "Pt", "kbgn"))
                sc, AqkT, kg, vb, WTn, qgT, Ptf = (ho[k] for k in ("sc", "AqkT", "kg", "vb", "WTn", "qgT", "Ptf"))
                KW = lambda *a: ("b_cs", n) + a
                KH = lambda *a: ("b_ho", hn) + a
                HN = ("sc", "AqkT", "kg", "vb", "WTn", "qgT", "Ptf")
                K = lambda *a: (KH(*a) if a[0] in HN else KW(*a))
                pGl, kGl = nxt()
                pG = pGl[:, 0, 0:8]
                P.I("pe", "matmul", reads=["cst", "b_g"], writes=[kGl], out=pG[:, 0:4], lhsT=UM, rhs=gg[:, t, :], start=True, stop=True)
                P.I("pe", "matmul", reads=["cst", "b_g"], writes=[kGl], out=pG[:, 4:8], lhsT=ONES, rhs=gg[:, t, :], start=True, stop=True)
                for h in H4:
                    P.I("dve", "tensor_scalar", reads=["cst", "b_g"], writes=[K("gU", h)], out=gU[:, h, :], in0=UM,
                        scalar1=gg[:, t, h:h + 1], scalar2=None, op0=ALU.mult)
                yield
                P.I("dve", "tensor_copy", reads=[kGl], writes=[K("Gt")], out=Gt[:, :], in_=pG[:, :])
                done(kGl)
                P.I("dve", "tensor_tensor", reads=[K("Gt")], writes=[K("sc", 1)], out=sc[:, 4:8], in0=Gt[:, 4:8], in1=Gt[:, 0:4], op=ALU.subtract)
                pGr, kGr = nxt()
                P.I("pe", "matmul", reads=[K("gU", h) for h in H4] + ["cst"], writes=[kGr], out=pGr[:, :, :], lhsT=ONES,
                    rhs=gU[:, :, :], start=True, stop=True)
                yield
                P.I("act", "activation", reads=[K("sc", 1)], writes=[K("sc", 1)], out=sc[:, 4:8], in_=sc[:, 4:8], func=AF.Exp)
                P.I("act", "activation", reads=[K("Gt")], writes=[K("sc", 2)], out=sc[:, 8:12], in_=Gt[:, 4:8], func=AF.Exp)
                P.I("act", "activation", reads=[K("Gt")], writes=[K("sc", 3)], out=sc[:, 12:16], in_=Gt[:, 0:4], func=AF.Exp)
                P.I("act", "activation", reads=[kGr], writes=[K("eGr")], out=eGr[:, :, :], in_=pGr[:, :, :], func=AF.Exp)
                for h in H4:
                    P.I("dve", "scalar_tensor_tensor", reads=[kGr, K("Gt"), "b_pmsl4"], writes=[K("tA", h)], out=tA[:, h, :],
                        in0=pGr[:, h, :], scalar=Gt[:, h:h + 1], in1=pmsl4[:, h, :], op0=ALU.subtract, op1=ALU.add)
                    P.I("dve", "scalar_tensor_tensor", reads=[kGr, K("Gt"), "b_mui4"], writes=[K("tAT", h)], out=tAT[:, h, :],
                        in0=pGr[:, h, :], scalar=Gt[:, h:h + 1], in1=mui4[:, h, :], op0=ALU.subtract, op1=ALU.add)
                done(kGr)
                yield
                P.I("dve", "tensor_tensor", reads=[K("sc", 3), "b_nbeta"], writes=[K("sc", 0)], out=sc[:, 0:4], in0=sc[:, 12:16],
                    in1=nbeta[:, t, :], op=ALU.mult)
                P.I("act", "activation", reads=[K("tA", h) for h in H4], writes=[K("Dm")], out=Dm[:, :, :], in_=tA[:, :, :], func=AF.Exp, scale=-1.0)
                P.I("act", "activation", reads=[K("tAT", h) for h in H4], writes=[K("DT")], out=DT[:, :, :], in_=tAT[:, :, :], func=AF.Exp)
                pKK, kKK = nxt()
                pQK, kQK = nxt()
                for h in H4:
                    P.I("pe", "matmul", reads=[QK[4 + h]], writes=[kKK], out=pKK[:, h, :], lhsT=qkb[:, 4 + h, cs], rhs=qkb[:, 4 + h, cs], start=True, stop=True)
                    P.I("pe", "matmul", reads=[QK[4 + h], QK[h]], writes=[kQK], out=pQK[:, h, :], lhsT=qkb[:, 4 + h, cs], rhs=qkb[:, h, cs], start=True, stop=True)
                yield
                x0 = X[0]
                for h in H4:
                    P.I("dve", "scalar_tensor_tensor", reads=[kKK, "b_nbeta", K("Dm")], writes=[K("X", 0, h)], out=x0[:, h, :],
                        in0=pKK[:, h, :], scalar=nbeta[:, t, h:h + 1], in1=Dm[:, h, :], op0=ALU.mult, op1=ALU.mult)
                done(kKK)
                P.I("dve", "tensor_tensor", reads=[kQK, K("DT")], writes=[K("AqkT")], out=AqkT[:, :, :], in0=pQK[:, :, :], in1=DT[:, :, :], op=ALU.mult)
                done(kQK)
                IDK = idb[:, :] if DK == BF16 else IDF
                pTk, kTk = nxt()
                vTk = bfv(pTk, DK)
                for h in H4:
                    P.I("pe", "transpose", reads=[QK[4 + h], "idb", "cst"], writes=[kTk], out=vTk[:, h, 0:128], in_=qkb[:, 4 + h, cs], identity=IDK)
                yield
                pXT, kXT = nxt()
                for h in H4:
                    P.I("pe", "transpose", reads=[K("X", 0, h), "cst"], writes=[kXT], out=pXT[:, h, :], in_=x0[:, h, :], identity=IDF)
                for h in H4:
                    P.I("dve", "tensor_scalar", reads=[kTk, K("sc", 0)], writes=[K("kbgn", h)], out=kbgn[:, h, :], in0=vTk[:, h, 0:128],
                        scalar1=sc[:, h:h + 1], scalar2=None, op0=ALU.mult)
                    P.I("dve", "tensor_scalar", reads=[kTk, K("sc", 1)], writes=[K("kg", h)], out=kg[:, h, :], in0=vTk[:, h, 0:128],
                        scalar1=sc[:, 4 + h:5 + h], scalar2=None, op0=ALU.mult)
                done(kTk)
                yield
                X0K = [K("X", 0, h) for h in H4]
                P.I("act", "copy", reads=[kXT], writes=[K("XT", 0)], out=XT[0][:, :, :], in_=pXT[:, :, :])
                P.I("dve", "tensor_tensor", reads=[kXT, "b_id4"], writes=[K("Pt", 0)], out=Pt[0][:, :, :], in0=pXT[:, :, :], in1=id4[:, :, :], op=ALU.add)
                done(kXT)
                pTv, kTv = nxt()
                vTv = bfv(pTv, DK)
                for h in H4:
                    P.I("pe", "transpose", reads=[QK[8 + h], "idb", "cst"], writes=[kTv], out=vTv[:, h, 0:128], in_=qkb[:, 8 + h, cs], identity=IDK)
                yield
                for h in H4:
                    P.I("dve", "tensor_scalar", reads=[kTv, "b_beta"], writes=[K("vb", h)], out=vb[:, h, :], in0=vTv[:, h, 0:128],
                        scalar1=beta[:, t, h:h + 1], scalar2=None, op0=ALU.mult)
                done(kTv)
                cur = 0
                pcur = 0
                xkeys = X0K
                for lv in range(1, 7):
                    nx = 1 - cur
                    pX2, kX2 = nxt()
                    for h in H4:
                        P.I("pe", "matmul", reads=xkeys + [K("XT", cur)], writes=[kX2], out=pX2[:, h, :], lhsT=XT[cur][:, h, :], rhs=X[cur][:, h, :], start=True, stop=True)
                    if lv < 6:
                        pXT2, kXT2 = nxt()
                        for h in H4:
                            P.I("pe", "matmul", reads=xkeys + [K("XT", cur)], writes=[kXT2], out=pXT2[:, h, :], lhsT=X[cur][:, h, :], rhs=XT[cur][:, h, :], start=True, stop=True)
                    yield
                    P.I("dve", "tensor_tensor", reads=[kX2, "b_id4"], writes=[K("IX")], out=IX[:, :, :], in0=pX2[:, :, :], in1=id4[:, :, :], op=ALU.add)
                    if lv < 6:
                        P.I("act", "copy", reads=[kX2], writes=[K("X", nx, h) for h in H4], out=X[nx][:, :, :], in_=pX2[:, :, :])
                        P.I("act", "copy", reads=[kXT2], writes=[K("XT", nx)], out=XT[nx][:, :, :], in_=pXT2[:, :, :])
                        done(kXT2)
                    done(kX2)
                    yield
                    pP, kP = nxt()
                    for h in H4:
                        P.I("pe", "matmul", reads=[K("IX"), K("Pt", pcur)], writes=[kP], out=pP[:, h, :], lhsT=IX[:, h, :], rhs=Pt[pcur][:, h, :], start=True, stop=True)
                    yield
                    if lv < 6:
                        P.I("dve", "tensor_copy", reads=[kP], writes=[K("Pt", 1 - pcur)], out=Pt[1 - pcur][:, :, :], in_=pP[:, :, :])
                    else:
                        P.I("dve", "tensor_copy", reads=[kP], writes=[K("Ptf")], out=Ptf[:, :, :], in_=pP[:, :, :])
                    done(kP)
                    pcur = 1 - pcur
                    cur = nx
                    xkeys = [K("X", cur, h) for h in H4]
                    yield
                PT_ = Ptf
                kPt = K("Ptf")
                pW, kW = nxt()
                for h in H4:
                    P.I("pe", "matmul", reads=[K("kbgn", h), kPt], writes=[kW], out=pW[:, h, :], lhsT=kbgn[:, h, :], rhs=PT_[:, h, :], start=True, stop=True)
                P.I("pool", "tensor_tensor", reads=QK[0:4] + [K("eGr")], writes=[K("qgT")], out=qgT[:, :, :], in0=qkb[:, 0:4, cs], in1=eGr[:, :, :], op=ALU.mult)
                yield
                P.I("act", "copy", reads=[kW], writes=[K("WTn")], out=WTn[:, :, :], in_=pW[:, :, :])
                done(kW)
                yield

            def chunk_scan(j, ho, t0=t0, s=s):
                t = t0 + j
                hn = ho["n"]
                cs = slice(j * 128, (j + 1) * 128)
                sz = szs[s % 3]
                sc, AqkT, kg, vb, WTn, qgT, PT_ = (ho[k] for k in ("sc", "AqkT", "kg", "vb", "WTn", "qgT", "Ptf"))
                K = lambda *a: ("b_ho", hn) + a
                kPt = K("Ptf")
                pV, kV = nxt()
                for h in H4:
                    P.I("pe", "matmul", reads=[kPt, K("vb", h)], writes=[kV], out=pV[:, h, :], lhsT=PT_[:, h, :], rhs=vb[:, h, :], start=True, stop=False)
                    P.I("pe", "matmul", reads=[K("WTn"), ("b_S", h)], writes=[kV], out=pV[:, h, :], lhsT=WTn[:, h, :], rhs=St[:, h, :], start=False, stop=True)
                yield
                P.I("dve", "tensor_copy", reads=[kV], writes=["b_vnew"], out=vnew[:, :, :], in_=pV[:, :, :])
                done(kV)
                yield
                pO, kO = nxt()
                for h in H4:
                    P.I("pe", "matmul", reads=["b_Sb", K("qgT")], writes=[kO], out=pO[:, h, :], lhsT=Sb[:, h, :], rhs=qgT[:, h, :], start=True, stop=False)
                    P.I("pe", "matmul", reads=["b_vnew", K("AqkT")], writes=[kO], out=pO[:, h, :], lhsT=vnew[:, h, :], rhs=AqkT[:, h, :], start=False, stop=True)
                pSn, kSn = nxt()
                for h in H4:
                    P.I("pe", "matmul", reads=[K("kg", h), "b_vnew"], writes=[kSn], out=pSn[:, h, :], lhsT=kg[:, h, :], rhs=vnew[:, h, :], start=True, stop=True)
                yield
                for h in H4:
                    P.I("dve", "scalar_tensor_tensor", reads=[("b_S", h), K("sc", 2), kSn], writes=[("b_S", h)], out=St[:, h, :],
                        in0=St[:, h, :], scalar=sc[:, 8 + h:9 + h], in1=pSn[:, h, :], op0=ALU.mult, op1=ALU.add)
                done(kSn)
                P.I("act", "copy", reads=[("b_S", h) for h in H4], writes=["b_Sb"], out=Sb[:, :, :], in_=St[:, :, :])
                yield
                P.I("act", "activation", reads=[kO], writes=["b_osq"], out=osq[:, :, :], in_=pO[:, :, :], func=AF.Square)
                yield
                pSS, kSS = nxt()
                P.I("pe", "matmul", reads=["oneb", "b_osq"], writes=[kSS], out=pSS[:, :, :], lhsT=oneb[:, :], rhs=osq[:, :, :], start=True, stop=True)
                yield
                P.I("act", "activation", reads=[kSS, "epsc"], writes=["b_rno"], out=rno[:, :, :], in_=pSS[:, :, :], func=AF.Ln,
                    bias=epsc[:, 0:1], scale=1.0 / 128)
                P.I("act", "activation", reads=["b_rno"], writes=["b_rno"], out=rno[:, :, :], in_=rno[:, :, :], func=AF.Exp, scale=-0.5)
                done(kSS)
                yield
                P.I("dve", "scalar_tensor_tensor", reads=[kO, "prm", "b_rno"], writes=["b_t1"], out=t1[:, :, :], in0=pO[:, :, :],
                    scalar=prm[:, P_GDN:P_GDN + 1], in1=rno[:, :, :], op0=ALU.mult, op1=ALU.mult)
                done(kO)
                obb = ob[t % 2]
                P.I("pool", "tensor_tensor", reads=["b_t1", ("b_sz", s % 3)], writes=[("b_ob", t % 2)], out=obb[:, :, :], in0=t1[:, :, :],
                    in1=sz[:, :, cs], op=ALU.mult)
                P.dma(env["oT"].ap()[4:8, :, t * 128:(t + 1) * 128].rearrange("k p c -> p k c"), obb[:, :, :],
                      reads=[("b_ob", t % 2)], writes=[("oT_dn", t)], sem=f"b_so{t % 2}", eng="pool")
                yield

            for j0 in range(0, ntile, 2):
                js = [j for j in (j0, j0 + 1) if j < ntile]
                pp_ = state_b["pair"] % 2
                state_b["pair"] += 1
                hos = [HO[2 * pp_ + i] for i in range(len(js))]

                def scan_chain(js=js, hos=hos, scan_fn=chunk_scan):
                    for i, j in enumerate(js):
                        yield from scan_fn(j, hos[i])

                gens = [chunk_prep(j, BS[i], hos[i]) for i, j in enumerate(js)] + nextconv
                nextconv = []
                if state_b["prev"] is not None:
                    gens.append(state_b["prev"])
                drive(gens)
                state_b["prev"] = scan_chain()
        drive([state_b["prev"]])


def phase_c(env):
    nc, P, NT, T = env["nc"], env["P"], env["NT"], env["T"]
    prm, cst, oneb, epsc = env["prm"], env["cst"], env["oneb"], env["epsc"]
    SCALE = 64 ** -0.5
    LAM_INIT = 0.8 - 0.6 * math.exp(-0.3 * 0)
    with ExitStack() as st:
        S = lambda name, shape, dt: st.enter_context(nc.sbuf_tensor(name, shape, dt))
        PS = lambda name, shape, dt: st.enter_context(nc.psum_tensor(name, shape, dt))
        relb_t = S("c_relb", [32, 4], F32)
        gv_t = S("c_gv", [4, 512], F32)
        lt = S("c_lt", [128, 128], F32)
        lsum = S("c_lsum", [128, 2], F32)
        neglam = S("c_neglam", [128, 1], F32)
        gsub = S("c_gsub", [128, 1], F32)
        qT = [[S(f"c_qT{p}{i}", [128, T], BF16) for i in range(2)] for p in range(2)]
        kT = [S(f"c_kT{p}", [128, T], BF16) for p in range(2)]
        vh = [S(f"c_vh{p}", [128, NT, 128], BF16) for p in range(2)]
        vmeta = [S(f"c_vmeta{p}", [16, 128], BF16) for p in range(2)]
        hk0 = [S(f"c_hk0{p}", [128, 128], F32) for p in range(2)]
        hk1 = [S(f"c_hk1{p}", [128, 128], F32) for p in range(2)]
        hkm = [S(f"c_hkm{p}", [16, 128], F32) for p in range(2)]
        B0 = [S(f"c_B0{p}", [128, 128], F32) for p in range(2)]
        B1 = [S(f"c_B1{p}", [128, 128], F32) for p in range(2)]
        Bm = [S(f"c_Bm{p}", [16, 128], F32) for p in range(2)]
        PT = [S(f"c_PT{i}", [128, 512], BF16) for i in range(4)]
        tmpS = [S(f"c_tmpS{i}", [128, 128], F32) for i in range(2)]
        oc = [[S(f"c_oc{p}{i}", [128, 512], F32) for i in range(2)] for p in range(2)]
        rz = [[S(f"c_rz{p}{i}", [128, 512], F32) for i in range(2)] for p in range(2)]
        osq = [S(f"c_osq{i}", [128, 512], BF16) for i in range(2)]
        rstd = [S(f"c_rstd{i}", [128, 512], F32) for i in range(2)]
        onb = [S(f"c_onb{i}", [128, 512], BF16) for i in range(2)]
        zpad = S("c_zpad", [128, 112], BF16)
        mpo = [S(f"c_mpo{i}", [128, 16], F32) for i in range(2)]
        mpz = [S(f"c_mpz{i}", [128, 16], F32) for i in range(2)]
        Pacc = [S(f"c_Pacc{i}", [128, 512], F32) for i in range(2)]
        pS = [PS(f"c_pS{i}", [128, 512], F32) for i in range(3)]
        pO = [PS(f"c_pO{i}", [128, 512], F32) for i in range(2)]
        pZ = [PS(f"c_pZ{i}", [128, 512], F32) for i in range(2)]
        pN = PS("c_pN", [128, 512], F32)
        pM = pN

        for i in range(2):
            a0 = P_LAM + 128 * i
            P.I("dve", "tensor_tensor", reads=["prm"], writes=["c_lt"], out=lt[:, 0:64], in0=prm[:, a0:a0 + 64],
                in1=prm[:, a0 + 64:a0 + 128], op=ALU.mult)
            P.I("dve", "reduce_sum", reads=["c_lt"], writes=[("c_lsum", i)], out=lsum[:, i:i + 1], in_=lt[:, 0:64],
                axis=mybir.AxisListType.X)
        P.I("act", "activation", reads=[("c_lsum", 0), ("c_lsum", 1)], writes=[("c_lsum", 0), ("c_lsum", 1)],
            out=lsum[:, :], in_=lsum[:, :], func=AF.Exp)
        P.I("dve", "scalar_tensor_tensor", reads=[("c_lsum", 0), ("c_lsum", 1)], writes=["c_neglam"],
            out=neglam[:, :], in0=lsum[:, 1:2], scalar=-LAM_INIT, in1=lsum[:, 0:1], op0=ALU.add, op1=ALU.subtract)
        P.I("dve", "tensor_scalar", reads=["prm"], writes=["c_gsub"], out=gsub[:, :], in0=prm[:, P_GSUB:P_GSUB + 1],
            scalar1=1.0 - LAM_INIT, scalar2=None, op0=ALU.mult)
        P.I("pool", "memset", writes=["c_zpad"], ap=zpad[:, :], constant=0.0)
        P.dma(relb_t[:, :], env["relb"].ap()[:, :], writes=["c_relb"], sem="l_relb")
        P.I("pe", "matmul", reads=["c_relb", "cst"], writes=["c_pN"], out=pN[0:4, :], lhsT=relb_t[:, :],
            rhs=cst[0:32, C_E1:C_E1 + 512], start=True, stop=True)
        P.I("dve", "tensor_copy", reads=["c_pN"], writes=["c_gv"], out=gv_t[:, :], in_=pN[0:4, :])
        P.dma(env["fvec"].ap()[:, :], gv_t[:, :], reads=["c_gv"], writes=["fvec"], sem="s_fvec")

        nq_tiles = NT - 1
        nsup = (nq_tiles + 3) // 4

        def head_loads(h):
            p = h % 2
            P.dma(qT[p][0][:, :], env["qT_da"].ap()[h, :, :], writes=[("c_qT", p, 0)], sem=f"c_l0{p}")
            P.dma(qT[p][1][:, :], env["qT_da"].ap()[h, :, :], writes=[("c_qT", p, 1)], sem=f"c_l7{p}")
            P.I("pool", "memset", writes=[("c_qT", p, 0)], ap=qT[p][0][64:128, :], constant=0.0)
            P.I("pool", "memset", writes=[("c_qT", p, 1)], ap=qT[p][1][0:64, :], constant=0.0)
            P.dma(kT[p][:, :], env["kT_da"].ap()[h, :, :], writes=[("c_kT", p)], sem=f"c_l1{p}")
            P.dma(vh[p][:, :, :], env["v_da"].ap()[:, h * 128:(h + 1) * 128].rearrange("(t p) d -> p t d", p=128),
                  writes=[("c_vh", p)], sem=f"c_l2{p}")
            P.dma(vmeta[p][:, :], env["v_da"].ap()[112:128, h * 128:(h + 1) * 128], writes=[("c_vmeta", p)], sem=f"c_l3{p}")
            P.dma(hk0[p][:, :], bass.AP(env["fvec"], h * 512 + 129, [[1, 128], [1, 128]]),
                  reads=["fvec"], writes=[("c_hk0", p)], sem=f"c_l4{p}")
            P.dma(hk1[p][:, :], bass.AP(env["fvec"], h * 512 + 257, [[1, 128], [1, 128]]),
                  reads=["fvec"], writes=[("c_hk1", p)], sem=f"c_l5{p}")
            P.dma(hkm[p][:, :], bass.AP(env["fvec"], h * 512 + 257, [[1, 16], [1, 128]]),
                  reads=["fvec"], writes=[("c_hkm", p)], sem=f"c_l6{p}")

        def head_bias(h, staged=True):
            p = h % 2
            c15h = prm[:, P_C15 + h:P_C15 + h + 1]

            def s0():
                P.I("pe", "matmul", reads=["cst", ("c_hk0", p)], writes=["c_pN"], out=pM[:, 0:128], lhsT=cst[:, C_J:C_J + 128],
                    rhs=hk0[p][:, :], start=True, stop=True)

            def s1():
                P.I("dve", "tensor_scalar", reads=["c_pN", "prm"], writes=[("c_B0", p)], out=B0[p][:, :], in0=pM[:, 0:128],
                    scalar1=c15h, scalar2=1.0 / SCALE, op0=ALU.subtract, op1=ALU.mult)
                P.I("dve", "memset", writes=[("c_B0", p)], ap=B0[p][64:128, 0:64], constant=-30000.0 / SCALE)

            def s2():
                P.I("pe", "matmul", reads=["cst", ("c_hk1", p)], writes=["c_pN"], out=pM[:, 0:128], lhsT=cst[:, C_J:C_J + 128],
                    rhs=hk1[p][:, :], start=True, stop=True)

            def s3():
                P.I("dve", "tensor_scalar", reads=["c_pN", "prm"], writes=[("c_B1", p)], out=B1[p][:, :], in0=pM[:, 0:128],
                    scalar1=c15h, scalar2=1.0 / SCALE, op0=ALU.subtract, op1=ALU.mult)

            def s4():
                P.I("pe", "matmul", reads=["cst", ("c_hkm", p)], writes=["c_pN"], out=pM[0:16, 0:128],
                    lhsT=cst[0:16, C_J + 112:C_J + 128], rhs=hkm[p][:, :], start=True, stop=True)

            def s5():
                P.I("dve", "tensor_scalar", reads=["c_pN", "prm"], writes=[("c_Bm", p)], out=Bm[p][:, :], in0=pM[0:16, 0:128],
                    scalar1=c15h[0:16, :], scalar2=1.0 / SCALE, op0=ALU.subtract, op1=ALU.mult)

            def s01():
                s0()
                s1()

            def s23():
                s2()
                s3()

            def s45():
                s4()
                s5()
                bias_done.add(h)

            stages = [s01, s23, s45]
            if not staged:
                for f in stages:
                    f()
            else:
                for i, f in enumerate(stages):
                    defer(2 + 3 * i, f)

        pending = []

        def defer(delay, fn):
            pending.append([delay, fn])

        def tick():
            due = []
            for ent in pending:
                ent[0] -= 1
                if ent[0] <= 0:
                    due.append(ent)
            for ent in due:
                pending.remove(ent)
            for ent in due:
                ent[1]()

        def flush_pending():
            while pending:
                tick()

        state = dict(pti=0, acc=0)
        norm_done = [True, True]
        bias_done = set()
        head_loads(0)
        head_bias(0, staged=False)
        bias_done.add(0)

        comps = []
        blkc = 0
        for h in range(4):
            hp = h % 2
            KT = kT[hp]
            VH = vh[hp]
            VM = vmeta[hp]
            RK = [("c_kT", hp), ("c_vh", hp), ("c_vmeta", hp), ("c_B0", hp), ("c_B1", hp), ("c_Bm", hp)]
            blocks = []
            blocks.append((112, 16, [(112, 16, VM[0:16, :], [(0, 16, "near", B0[hp][0:16, 0:16])])]))
            for qs in range(nsup):
                qt0 = 1 + 4 * qs
                nq = min(4, NT - qt0)
                Wq = nq * 128
                items = []
                segs = []
                if qs == 0:
                    segs.append((0, 128, "near", Bm[hp][0:16, :]))
                    if Wq > 128:
                        segs.append((128, Wq, "far", None))
                else:
                    segs.append((0, Wq, "far", None))
                items.append((112, 16, VM[0:16, :], segs))
                for kt in range(1, qt0 + nq):
                    jmin = max(0, kt - qt0)
                    segs = []
                    j = jmin
                    if kt == qt0 + j:
                        segs.append((128 * j, 128 * j + 128, "near", B0[hp][:, :]))
                        j += 1
                    if j < nq and kt == qt0 + j - 1:
                        segs.append((128 * j, 128 * j + 128, "near", B1[hp][:, :]))
                        j += 1
                    if j < nq:
                        segs.append((128 * j, Wq, "far", None))
                    items.append((kt * 128, 128, VH[:, kt, :], segs))
                blocks.append((qt0 * 128, Wq, items))
            for bi, (qc0, Wq, items) in enumerate(blocks):
                for c in range(2):
                    comps.append(dict(h=h, hp=hp, bi=bi, nblocks=len(blocks), c=c, qc0=qc0, Wq=Wq, items=items,
                                      KT=KT, RK=RK, bp=blkc % 2, c15=prm[:, P_C15 + h:P_C15 + h + 1]))
                blkc += 1
        flat = [(ci, i) for ci, cm in enumerate(comps) for i in range(len(cm["items"]))]

        def rec_S(f):
            ci, i = flat[f]
            cm = comps[ci]
            if cm["h"] not in bias_done:
                flush_pending()
            kc0, M, vl, segs = cm["items"][i]
            lo = segs[0][0]
            b = f % 3
            hp, c, Wq, qc0 = cm["hp"], cm["c"], cm["Wq"], cm["qc0"]
            nears = [sg for sg in segs if sg[2] == "near"]
            P.I("pe", "matmul", reads=[("c_kT", hp), ("c_qT", hp, c)], writes=[("c_pS", b)],
                out=pS[b][0:M, lo:Wq], lhsT=cm["KT"][:, kc0:kc0 + M],
                rhs=qT[hp][c][:, qc0 + lo:qc0 + Wq], start=True, stop=(len(nears) == 0))
            for ni, (c0, c1, kind, bap) in enumerate(nears):
                P.I("pe", "matmul", reads=["cst"] + cm["RK"], writes=[("c_pS", b)],
                    out=pS[b][0:M, c0:c1], lhsT=cst[0:M, C_ID:C_ID + M], rhs=bap, start=False, stop=(ni == len(nears) - 1))

        def chain(c, bp, Wq, PO, PZ, kPO, kPZ, qc0, h, ai, PA, kPA, n):
            RZ = rz[bp][c]
            krz = ("c_rz", bp, c)
            O0 = oc[bp][0]

            def n_z():
                P.I("pe", "matmul", reads=[kPA, "cst"], writes=[kPZ],
                    out=PZ[:, 0:Wq], lhsT=cst[:, C_ONE:C_ONE + 128], rhs=PA[:, 0:Wq], start=False, stop=True)
                defer(2, n_a)

            def n_a():
                P.I("act", "activation", reads=[kPZ], writes=[krz], out=RZ[:, 0:Wq], in_=PZ[:, 0:Wq], func=AF.Ln)
                defer(1, n_b)

            def n_b():
                P.I("act", "activation", reads=[krz], writes=[krz], out=RZ[:, 0:Wq], in_=RZ[:, 0:Wq], func=AF.Exp, scale=-1.0)
                P.I("dve", "tensor_tensor", reads=[kPO, krz], writes=[("c_oc", bp, c)],
                    out=oc[bp][c][:, 0:Wq], in0=PO[:, 0:Wq], in1=RZ[:, 0:Wq], op=ALU.mult)
                if ai is not None:
                    norm_done[ai] = True
                if c == 1:
                    defer(1, e_a)

            def e_a():
                P.I("dve", "scalar_tensor_tensor", reads=[("c_oc", bp, 0), ("c_oc", bp, 1), "c_neglam"], writes=[("c_oc", bp, 0)],
                    out=O0[:, 0:Wq], in0=oc[bp][1][:, 0:Wq], scalar=neglam[:, 0:1], in1=O0[:, 0:Wq], op0=ALU.mult, op1=ALU.add)
                P.I("dve", "tensor_tensor", reads=[("c_oc", bp, 0)], writes=[("c_osq", bp)],
                    out=osq[bp][:, 0:Wq], in0=O0[:, 0:Wq], in1=O0[:, 0:Wq], op=ALU.mult)
                defer(4, e_b)

            def e_b():
                P.I("pe", "matmul", reads=["oneb", ("c_osq", bp)], writes=["c_pN"],
                    out=pN[:, 0:Wq], lhsT=oneb[:, :], rhs=osq[bp][:, 0:Wq], start=True, stop=True)
                P.I("act", "activation", reads=["c_pN", "epsc"], writes=[("c_rstd", bp)],
                    out=rstd[bp][:, 0:Wq], in_=pN[:, 0:Wq], func=AF.Ln, bias=epsc[:, 0:1], scale=1.0 / 128)
                defer(1, e_d)

            def e_d():
                P.I("act", "activation", reads=[("c_rstd", bp)], writes=[("c_rstd", bp)],
                    out=rstd[bp][:, 0:Wq], in_=rstd[bp][:, 0:Wq], func=AF.Exp, scale=-0.5)
                P.I("dve", "scalar_tensor_tensor", reads=[("c_oc", bp, 0), "c_gsub", ("c_rstd", bp)], writes=[("c_onb", bp)],
                    out=onb[bp][:, 0:Wq], in0=O0[:, 0:Wq], scalar=gsub[:, 0:1], in1=rstd[bp][:, 0:Wq], op0=ALU.mult, op1=ALU.mult)
                P.dma(env["oT"].ap()[h, :, qc0:qc0 + Wq], onb[bp][:, 0:Wq], reads=[("c_onb", bp)],
                      writes=[("oT", h, qc0)], sem=f"c_st{bp}")

            if n > 1:
                defer(1, n_z)
            else:
                defer(2, n_a)


        rec_S(0)
        if len(flat) > 1:
            rec_S(1)
        ctx = {}
        for f, (ci, i) in enumerate(flat):
            cm = comps[ci]
            h, hp, bi, c, qc0, Wq, items, RK, bp, c15 = (cm[k] for k in ("h", "hp", "bi", "c", "qc0", "Wq", "items", "RK", "bp", "c15"))
            n = len(items)
            if i == 0:
                if c == 0:
                    nb = cm["nblocks"]
                    if h == 0 and bi == min(3, nb - 1):
                        load_weight_bf16(P, nc, env["wup"], env["w_up"], 8, 2 * D_FF, "e_wup", "wE")
                    if bi == min(2, nb - 1) and h + 1 < 4:
                        head_loads(h + 1)
                    if bi == min(4, nb - 1) and h + 1 < 4:
                        head_bias(h + 1)
                if n == 1:
                    bb = f % 3
                    ctx = dict(ai=None, PO=pS[bb][:, 256:272], PZ=pS[bb][:, 288:304], kPO=("c_pS", bb), kPZ=("c_pS", bb),
                               PA=Pacc[0], kPA=("c_Pacc", 0))
                else:
                    ai = state["acc"] % 2
                    state["acc"] += 1
                    if not norm_done[ai]:
                        flush_pending()
                    ctx = dict(ai=ai, PO=pO[ai], PZ=pZ[ai], kPO=("c_pO", ai), kPZ=("c_pZ", ai), PA=Pacc[ai], kPA=("c_Pacc", ai))
            ai, PO, PZ, kPO, kPZ, PA, kPA = (ctx[k] for k in ("ai", "PO", "PZ", "kPO", "kPZ", "PA", "kPA"))
            kc0, M, vl, segs = items[i]
            lo = segs[0][0]
            b = f % 3
            if f + 2 < len(flat):
                rec_S(f + 2)
            pt = PT[state["pti"] % 4]
            ptk = ("c_PT", state["pti"] % 4)
            state["pti"] += 1
            P.I("act", "activation", reads=[("c_pS", b), "prm"], writes=[ptk],
                out=pt[0:M, lo:Wq], in_=pS[b][0:M, lo:Wq], func=AF.Exp, bias=c15[0:M, :], scale=SCALE)
            P.I("pe", "matmul", reads=[ptk] + RK, writes=[kPO],
                out=PO[:, lo:Wq], lhsT=vl, rhs=pt[0:M, lo:Wq], start=(i == 0), stop=(i == n - 1))
            if i == 0:
                P.I("pe", "matmul", reads=[ptk, "oneb"], writes=[kPZ],
                    out=PZ[:, lo:Wq], lhsT=oneb[0:M, :], rhs=pt[0:M, lo:Wq], start=True, stop=(n == 1))
            elif i == 1:
                P.I("dve", "tensor_copy", reads=[ptk], writes=[kPA], out=PA[:, lo:Wq], in_=pt[:, lo:Wq])
            else:
                P.I("dve", "tensor_tensor", reads=[ptk, kPA], writes=[kPA], out=PA[:, lo:Wq], in0=PA[:, lo:Wq],
                    in1=pt[:, lo:Wq], op=ALU.add)
            tick()
            if i == n - 1:
                if n == 1:
                    P.I("dve", "tensor_copy", reads=[kPO], writes=[("c_mpo", c)], out=mpo[c][:, 0:Wq], in_=PO[:, 0:Wq])
                    P.I("dve", "tensor_copy", reads=[kPZ], writes=[("c_mpz", c)], out=mpz[c][:, 0:Wq], in_=PZ[:, 0:Wq])
                    chain(c, bp, Wq, mpo[c], mpz[c], ("c_mpo", c), ("c_mpz", c), qc0, h, None, PA, kPA, n)
                else:
                    norm_done[ai] = False
                    chain(c, bp, Wq, PO, PZ, kPO, kPZ, qc0, h, ai, PA, kPA, n)
                if c == 1 and bi == cm["nblocks"] - 1:
                    P.dma(env["oT"].ap()[h, :, 0:112], zpad[:, :], reads=["c_zpad"], writes=[("oT", h, 0)], sem="c_st2")
        flush_pending()


def phase_d(env):
    nc, P, NT, T = env["nc"], env["P"], env["NT"], env["T"]
    idb, epsc = env["idb"], env["epsc"]
    with ExitStack() as st:
        S = lambda name, shape, dt: st.enter_context(nc.sbuf_tensor(name, shape, dt))
        PS = lambda name, shape, dt: st.enter_context(nc.psum_tensor(name, shape, dt))
        wout = S("d_wout", [128, 8, D], BF16)
        g2 = S("d_g2", [128, D], F32)
        ot = [S(f"d_ot{i}", [128, 8, 128], BF16) for i in range(2)]
        ht = [S(f"d_h{i}", [128, D], F32) for i in range(2)]
        hm = [S(f"d_hm{i}", [128, D], F32) for i in range(2)]
        junk = S("d_junk", [128, D], BF16)
        ss = [S(f"d_ss{i}", [128, 1], F32) for i in range(2)]
        rstd = [S(f"d_rs{i}", [128, 1], F32) for i in range(2)]
        ub = [S(f"d_u{i}", [128, D], BF16) for i in range(2)]
        ut = [S(f"d_ut{i}", [128, 8, 128], BF16) for i in range(2)]
        pm = [PS(f"d_pm{i}", [128, 512], F32) for i in range(4)]
        pT = [PS(f"d_pT{i}", [128, 8, 128], BF16) for i in range(2)]
        P.dma(g2[:, :], env["gb"].ap()[:, D:2 * D], writes=["d_g2"], sem="l_g2")
        load_weight_bf16(P, nc, wout, env["w_out"], 8, D, "d_wout", "wD")
        WK = ["d_wout"] * 8

        def load(t):
            b = t % 2
            P.dma(ot[b][:, :, :], env["oT"].ap()[:, :, t * 128:(t + 1) * 128].rearrange("k p c -> p k c"),
                  writes=[("d_ot", b)], sem=f"d_lo{b}")
            P.dma(ht[b][:, :], env["hin"].ap()[t * 128:(t + 1) * 128, :], writes=[("d_h", b)], sem=f"d_lh{b}")

        def mm(t):
            b = t % 2
            for half in range(2):
                pb = (2 * t + half) % 4
                for kc in range(8):
                    P.I("pe", "matmul", reads=[("d_ot", b), WK[kc]], writes=[("d_pm", pb)],
                        out=pm[pb][:, :], lhsT=ot[b][:, kc, :], rhs=wout[:, kc, half * 512:(half + 1) * 512],
                        start=(kc == 0), stop=(kc == 7))

        load(0)
        if NT > 1:
            load(1)
        mm(0)
        for t in range(NT):
            b = t % 2
            if t + 1 < NT:
                mm(t + 1)
            for half in range(2):
                pb = (2 * t + half) % 4
                P.I("dve", "tensor_tensor", reads=[("d_pm", pb), ("d_h", b)], writes=[("d_hm", b, half)],
                    out=hm[b][:, half * 512:(half + 1) * 512], in0=pm[pb][:, :],
                    in1=ht[b][:, half * 512:(half + 1) * 512], op=ALU.add)
            if t + 2 < NT:
                load(t + 2)
            HM = [("d_hm", b, 0), ("d_hm", b, 1)]
            P.dma(env["hmid"].ap()[t * 128:(t + 1) * 128, :], hm[b][:, :], reads=HM, writes=[("hmid", t)], sem=f"d_sh{b}", eng="pool")
            P.I("act", "activation", reads=HM, writes=["d_junk", ("d_ss", b)],
                out=junk[:, :], in_=hm[b][:, :], func=AF.Square, accum_out=ss[b][:, :])
            P.I("act", "activation", reads=[("d_ss", b), "epsc"], writes=[("d_rs", b)],
                out=rstd[b][:, :], in_=ss[b][:, :], func=AF.Sqrt, bias=epsc[:, 0:1], scale=1.0 / D)
            P.I("dve", "reciprocal", reads=[("d_rs", b)], writes=[("d_rs", b)], out=rstd[b][:, :], in_=rstd[b][:, :])
            P.I("dve", "scalar_tensor_tensor", reads=HM + [("d_rs", b), "d_g2"], writes=[("d_u", b)],
                out=ub[b][:, :], in0=hm[b][:, :], scalar=rstd[b][:, 0:1], in1=g2[:, :], op0=ALU.mult, op1=ALU.mult)
            for kc in range(8):
                P.I("pe", "transpose", reads=[("d_u", b), "idb"], writes=[("d_pT", b)],
                    out=pT[b][:, kc, :], in_=ub[b][:, kc * 128:(kc + 1) * 128], identity=idb[:, :])
            P.I("act", "copy", reads=[("d_pT", b)], writes=[("d_ut", b)], out=ut[b][:, :, :], in_=pT[b][:, :, :])
            P.dma(env["u2T"].ap()[:, :, t * 128:(t + 1) * 128].rearrange("k p c -> p k c"), ut[b][:, :, :],
                  reads=[("d_ut", b)], writes=[("u2T", t)], sem=f"d_su{b}", eng="pool")


def phase_e(env):
    nc, P, NT, T = env["nc"], env["P"], env["NT"], env["T"]
    prm, epsc = env["prm"], env["epsc"]
    WIN = 384
    with ExitStack() as st:
        S = lambda name, shape, dt: st.enter_context(nc.sbuf_tensor(name, shape, dt))
        PS = lambda name, shape, dt: st.enter_context(nc.psum_tensor(name, shape, dt))
        wup = env["wup"]
        wdn = S("e_wdn", [128, 22, D], BF16)
        gf = S("e_gf", [128, D], F32)
        u2w = [S(f"e_u2w{i}", [128, 8, WIN + 2], BF16) for i in range(2)]
        actT = S("e_actT", [128, 22, WIN], BF16)
        yg = [S(f"e_yg{i}", [128, WIN], F32) for i in range(2)]
        yv = [S(f"e_yv{i}", [128, WIN], F32) for i in range(2)]
        sg = [S(f"e_sg{i}", [128, WIN], F32) for i in range(2)]
        hm = [S(f"e_hm{i}", [128, D], F32) for i in range(2)]
        ho = S("e_ho", [128, D], F32)
        yo = [S(f"e_yo{i}", [128, D], F32) for i in range(2)]
        junk = S("e_junk", [128, D], BF16)
        ss = S("e_ss", [128, 1], F32)
        rstd = S("e_rs", [128, 1], F32)
        pg = [PS(f"e_pg{i}", [128, 512], F32) for i in range(2)]
        pv = [PS(f"e_pv{i}", [128, 512], F32) for i in range(2)]
        pd = [PS(f"e_pd{i}", [128, 512], F32) for i in range(2)]
        P.dma(gf[:, :], env["gb"].ap()[:, 2 * D:3 * D], writes=["e_gf"], sem="l_gf")
        load_weight_bf16(P, nc, wdn, env["w_down"], 22, D, "e_wdn", "wE2", kgroup=6)
        UK = ["e_wup"] * 8

        wins = []
        a = 128
        while a < T:
            Ww = min(WIN, T - a)
            wins.append((a, Ww))
            a += Ww

        def loadw(wi):
            a, Ww = wins[wi]
            b = wi % 2
            P.dma(u2w[b][:, :, 0:Ww + 2], env["u2T"].ap()[:, :, a - 2:a + Ww].rearrange("k p c -> p k c"),
                  writes=[("e_u2w", b)], sem=f"e_lu{b}")

        loadw(0)
        it = 0
        tcount = 0
        for wi, (a, Ww) in enumerate(wins):
            b = wi % 2
            if wi + 1 < len(wins):
                loadw(wi + 1)
            for i in range(22):
                pb = it % 2
                it += 1
                for (ps_, col, key) in ((pg[pb], i * 128, ("e_pg", pb)), (pv[pb], D_FF + i * 128, ("e_pv", pb))):
                    for kc in range(8):
                        P.I("pe", "matmul", reads=[("e_u2w", b), UK[kc]], writes=[key],
                            out=ps_[:, 0:Ww + 2], lhsT=wup[:, kc, col:col + 128], rhs=u2w[b][:, kc, 0:Ww + 2],
                            start=(kc == 0), stop=(kc == 7))
                for (ps_, key, y, ykey, ct) in ((pg[pb], ("e_pg", pb), yg[pb], ("e_yg", pb), i),
                                                (pv[pb], ("e_pv", pb), yv[pb], ("e_yv", pb), 22 + i)):
                    w0 = prm[:, P_FW + ct * 3 + 0:P_FW + ct * 3 + 1]
                    w1 = prm[:, P_FW + ct * 3 + 1:P_FW + ct * 3 + 2]
                    w2 = prm[:, P_FW + ct * 3 + 2:P_FW + ct * 3 + 3]
                    bb = prm[:, P_FB + ct:P_FB + ct + 1]
                    P.I("act", "activation", reads=[key, "prm"], writes=[ykey],
                        out=y[:, 0:Ww], in_=ps_[:, 2:Ww + 2], func=AF.Identity, bias=bb, scale=w2)
                    P.I("dve", "scalar_tensor_tensor", reads=[key, "prm", ykey], writes=[ykey],
                        out=y[:, 0:Ww], in0=ps_[:, 1:Ww + 1], scalar=w1, in1=y[:, 0:Ww], op0=ALU.mult, op1=ALU.add)
                    P.I("dve", "scalar_tensor_tensor", reads=[key, "prm", ykey], writes=[ykey],
                        out=y[:, 0:Ww], in0=ps_[:, 0:Ww], scalar=w0, in1=y[:, 0:Ww], op0=ALU.mult, op1=ALU.add)
                P.I("act", "activation", reads=[("e_yg", pb)], writes=[("e_sg", pb)],
                    out=sg[pb][:, 0:Ww], in_=yg[pb][:, 0:Ww], func=AF.Silu)
                P.I("pool", "tensor_tensor", reads=[("e_sg", pb), ("e_yv", pb)], writes=[("e_actT", i)],
                    out=actT[:, i, 0:Ww], in0=sg[pb][:, 0:Ww], in1=yv[pb][:, 0:Ww], op=ALU.mult)
            AK = [("e_actT", i) for i in range(22)]
            for j in range(Ww // 128):
                tb = tcount % 2
                tcount += 1
                row0 = a + j * 128
                P.dma(hm[tb][:, :], env["hmid"].ap()[row0:row0 + 128, :], writes=[("e_hm", tb)], sem=f"e_lh{tb}")
                for half in range(2):
                    for i in range(22):
                        P.I("pe", "matmul", reads=[AK[i], "e_wdn"], writes=[("e_pd", half)],
                            out=pd[half][:, :], lhsT=actT[:, i, j * 128:(j + 1) * 128],
                            rhs=wdn[:, i, half * 512:(half + 1) * 512], start=(i == 0), stop=(i == 21))
                    P.I("dve", "tensor_tensor", reads=[("e_pd", half), ("e_hm", tb)], writes=[("e_ho", half)],
                        out=ho[:, half * 512:(half + 1) * 512], in0=pd[half][:, :],
                        in1=hm[tb][:, half * 512:(half + 1) * 512], op=ALU.add)
                HO = [("e_ho", 0), ("e_ho", 1)]
                P.I("act", "activation", reads=HO, writes=["e_junk", "e_ss"],
                    out=junk[:, :], in_=ho[:, :], func=AF.Square, accum_out=ss[:, :])
                P.I("act", "activation", reads=["e_ss", "epsc"], writes=["e_rs"],
                    out=rstd[:, :], in_=ss[:, :], func=AF.Sqrt, bias=epsc[:, 0:1], scale=1.0 / D)
                P.I("dve", "reciprocal", reads=["e_rs"], writes=["e_rs"], out=rstd[:, :], in_=rstd[:, :])
                P.I("dve", "scalar_tensor_tensor", reads=HO + ["e_rs", "e_gf"], writes=[("e_yo", tb)],
                    out=yo[tb][:, :], in0=ho[:, :], scalar=rstd[:, 0:1], in1=gf[:, :], op0=ALU.mult, op1=ALU.mult)
                P.dma(env["out"].ap()[row0 - 128:row0, :], yo[tb][:, :], reads=[("e_yo", tb)],
                      writes=[("out", row0)], sem=f"e_so{tb}")


def _t5_bucket_np(rel):
    nb = 16
    max_exact = 8
    rel = np.asarray(rel, np.int32)
    ret = np.where(rel > 0, nb, 0)
    n = np.abs(rel)
    nf = np.maximum(n, 1).astype(np.float32)
    large = max_exact + (np.log(nf / np.float32(max_exact)) / np.float32(math.log(128 / max_exact))
                         * np.float32(nb - max_exact)).astype(np.int32)
    large = np.minimum(large, nb - 1)
    return ret + np.where(n < max_exact, n, large)


def make_consts():
    c = np.zeros((128, NCST), np.float32)
    i = np.arange(128)
    c[:, C_ID:C_ID + 128] = np.eye(128, dtype=np.float32)
    c[:, C_J:C_J + 128] = np.eye(128, dtype=np.float32)[::-1]
    c[:, C_U:C_U + 128] = (i[:, None] <= i[None, :]).astype(np.float32)
    c[:, C_MSL:C_MSL + 128] = np.where(i[None, :] < i[:, None], 0.0, NEG)
    c[:, C_MUI:C_MUI + 128] = np.where(i[None, :] >= i[:, None], 0.0, NEG)
    c[:, C_ONE:C_ONE + 128] = 1.0
    c[:, C_PMSL:C_PMSL + 128] = np.where(i[None, :] < i[:, None], 0.0, -NEG)
    r = np.arange(512)
    b = _t5_bucket_np(256 - r)
    c[b, C_E1 + r] = 1.0
    return c


def make_prm(inp):
    p = np.zeros((128, NPRM), np.float32)
    p[:, P_GSUB] = inp["da_subln_g"][0]
    p[:, P_GDN] = inp["dn_norm_g"][0]
    p[:, P_ALOG:P_ALOG + 4] = inp["dn_A_log"][0][None, :]
    p[:, P_DTB:P_DTB + 4] = inp["dn_dt_bias"][0][None, :]
    p[:, P_C15:P_C15 + 4] = inp["rel_bias"][15][None, :]
    cw = inp["dn_conv_w"][0]
    p[:, P_CW:P_CW + 48] = cw.reshape(4, 12, 128).transpose(2, 1, 0).reshape(128, 48)
    fw = inp["ffn_conv_w"][0]
    p[:, P_FW:P_FW + 132] = fw.reshape(3, 44, 128).transpose(2, 1, 0).reshape(128, 132)
    p[:, P_FB:P_FB + 44] = inp["ffn_conv_b"][0].reshape(44, 128).T
    p[:, P_LAM:P_LAM + 256] = inp["da_lambda"][0].reshape(1, 256)
    return p


def make_gb(inp):
    g = np.zeros((128, 3 * D), np.float32)
    g[:, 0:D] = inp["norm1_g"][0][None, :]
    g[:, D:2 * D] = inp["norm2_g"][0][None, :]
    g[:, 2 * D:] = inp["final_norm_g"][None, :]
    return g


def make_hin(xb, meta, NT):
    h = np.zeros((NT * 128, D), np.float32)
    h[112:128] = meta
    h[128:] = xb[:(NT - 1) * 128]
    return h


_NC_CACHE = {}


def kernel(**inp):
    inp = {k: np.asarray(v) for k, v in inp.items()}
    x = inp["x"]
    B = x.shape[0]
    NT = NT_FULL
    if NT not in _NC_CACHE:
        _NC_CACHE[NT] = build(NT)
    nc = _NC_CACHE[NT]
    cst = make_consts()
    prm = make_prm(inp)
    gb = make_gb(inp)
    shared = dict(w_in=np.ascontiguousarray(inp["w_in"][0]), w_out=np.ascontiguousarray(inp["w_out"][0]),
                  w_up=np.ascontiguousarray(inp["w_up"][0]), w_down=np.ascontiguousarray(inp["w_down"][0]),
                  gb=gb, prm=prm, cst=cst, relb=np.ascontiguousarray(inp["rel_bias"]))
    in_maps = []
    for b in range(B):
        m = dict(shared)
        m["hin"] = make_hin(x[b], inp["meta_tokens"], NT)
        in_maps.append(m)
    res = run_bass_kernel_spmd(nc, in_maps, core_ids=list(range(B)))
    return np.stack([np.asarray(r["out"]) for r in res.results], axis=0).astype(np.float32)
```
